# Optimizing a Trainium2 kernel written in Bass

```python
import jax, jax.numpy as jnp
from jax import lax
import numpy as np

D_MODEL = 1024
BATCH = 8
SEQ = 2048
DEPTH = 2
DEC_BATCH = 128
DEC_SEQ = 8
PAST_LEN = 16384
PAGE_SIZE = 128

N_META = 16
D_CONV = D_MODEL
CONV_W = 31
D_POOL = D_MODEL
POOL_WINDOWS = (2, 4, 8, 16)
N_POOL_GROUPS = len(POOL_WINDOWS)
POOL_GC = D_POOL // N_POOL_GROUPS
POOL_MAX = max(POOL_WINDOWS)
D_FF = ((8 * D_MODEL // 3 + 127) // 128) * 128
FFN_CONV_W = 3
D_IN = 2 * D_CONV + D_POOL + 2 * D_MODEL
EPS = 1e-6

kernel_name = "hybrid_conformer_pool_convffn_decoder_step"


def rmsnorm(x, g):
    xf = x.astype(jnp.float32)
    y = xf * lax.rsqrt(jnp.mean(xf * xf, axis=-1, keepdims=True) + EPS)
    return (y * g.astype(jnp.float32)).astype(x.dtype)


def layernorm(x, g, b):
    xf = x.astype(jnp.float32)
    mu = jnp.mean(xf, axis=-1, keepdims=True)
    xc = xf - mu
    var = jnp.mean(xc * xc, axis=-1, keepdims=True)
    y = xc * lax.rsqrt(var + EPS) * g.astype(jnp.float32) + b.astype(jnp.float32)
    return y.astype(x.dtype)


def causal_dwconv(u, prev, w):
    k = w.shape[0]
    c = u.shape[-1]
    ext = jnp.concatenate([prev.astype(u.dtype), u], axis=1)
    y = lax.conv_general_dilated(
        ext, w.astype(ext.dtype)[:, None, :], window_strides=(1,), padding="VALID",
        dimension_numbers=("NWC", "WIO", "NWC"), feature_group_count=c)
    return y, ext[:, ext.shape[1] - (k - 1):]


def multiscale_pool(u, prev, pos0):
    b, t, _ = u.shape
    lp = POOL_MAX - 1
    ext = jnp.concatenate([prev.astype(u.dtype), u], axis=1)
    extf = ext.astype(jnp.float32)
    csum = jnp.concatenate([jnp.zeros((b, 1, extf.shape[-1]), jnp.float32),
                            lax.cumsum(extf, axis=1)], axis=1)
    pos = pos0 + jnp.arange(t, dtype=jnp.int32)
    means = []
    for gi, win in enumerate(POOL_WINDOWS):
        sl = slice(gi * POOL_GC, (gi + 1) * POOL_GC)
        s = csum[:, lp + 1:lp + 1 + t, sl] - csum[:, lp + 1 - win:lp + 1 - win + t, sl]
        cnt = jnp.minimum(pos + 1, win).astype(jnp.float32)
        means.append(s / cnt[None, :, None])
    mean = jnp.concatenate(means, axis=-1)
    y = (mean - u.astype(jnp.float32)).astype(u.dtype)
    return y, ext[:, ext.shape[1] - lp:]


def layer(x, prev_conv, prev_pool, prev_ffn, pos0, norm1_g, w_in, conv_dw, conv_b,
          ln_g, ln_b, w_conv_out, w_pool, pool_scale, w_out, norm2_g, w_up, ffn_dw, w_down):
    b, t, _ = x.shape
    h = rmsnorm(x, norm1_g)
    z = h @ w_in
    o1, o2, o3, o4 = D_CONV, 2 * D_CONV, 2 * D_CONV + D_POOL, 2 * D_CONV + D_POOL + D_MODEL
    za, zg, zp = z[..., :o1], z[..., o1:o2], z[..., o2:o3]
    gate_a, gate_b = z[..., o3:o4], z[..., o4:]
    a = za * jax.nn.sigmoid(zg)
    a, new_conv = causal_dwconv(a, prev_conv, conv_dw)
    a = layernorm(a + conv_b, ln_g, ln_b)
    a = jax.nn.silu(a) @ w_conv_out
    p, new_pool = multiscale_pool(zp, prev_pool, pos0)
    p = jnp.einsum("btgc,gcd->btgd", p.reshape(b, t, N_POOL_GROUPS, POOL_GC), w_pool)
    p = p.reshape(b, t, D_POOL) * pool_scale
    m = jax.nn.sigmoid(gate_a) * a + jax.nn.sigmoid(gate_b) * p
    x = x + m @ w_out
    h2 = rmsnorm(x, norm2_g)
    up = h2 @ w_up
    up, new_ffn = causal_dwconv(up, prev_ffn, ffn_dw)
    v, g = up[..., :D_FF], up[..., D_FF:]
    x = x + (jax.nn.gelu(g) * v) @ w_down
    return x, new_conv, new_pool, new_ffn


def run_trunk(x, st_conv, st_pool, st_ffn, pos0, norm1_g, w_in, conv_dw, conv_b, ln_g, ln_b,
              w_conv_out, w_pool, pool_scale, w_out, norm2_g, w_up, ffn_dw, w_down, final_g):
    ncs, nps, nfs = [], [], []
    for l in range(DEPTH):
        x, nc, np_, nf = layer(x, st_conv[l], st_pool[l], st_ffn[l], pos0,
                               norm1_g[l], w_in[l], conv_dw[l], conv_b[l], ln_g[l], ln_b[l],
                               w_conv_out[l], w_pool[l], pool_scale[l], w_out[l],
                               norm2_g[l], w_up[l], ffn_dw[l], w_down[l])
        ncs.append(nc); nps.append(np_); nfs.append(nf)
    return rmsnorm(x, final_g), jnp.stack(ncs), jnp.stack(nps), jnp.stack(nfs)


def setup_inputs(seed: int = 0) -> dict:
    key = jax.random.key(seed)
    ks = jax.random.split(key, 24)
    f32 = jnp.float32
    nrm = lambda k, shape, s: (jax.random.normal(k, shape, f32) * s)
    return {
        "x_prompt": nrm(ks[0], (BATCH, SEQ, D_MODEL), 1.0),
        "x_sample": nrm(ks[1], (DEC_BATCH, DEC_SEQ, D_MODEL), 1.0),
        "state_conv": nrm(ks[2], (DEPTH, DEC_BATCH, CONV_W - 1, D_CONV), 0.5),
        "state_pool": nrm(ks[3], (DEPTH, DEC_BATCH, POOL_MAX - 1, D_POOL), 1.0),
        "state_ffn": nrm(ks[4], (DEPTH, DEC_BATCH, FFN_CONV_W - 1, 2 * D_FF), 1.0),
        "meta_tokens": nrm(ks[5], (N_META, D_MODEL), 1.0),
        "norm1_g": 1.0 + nrm(ks[6], (DEPTH, D_MODEL), 0.02),
        "w_in": nrm(ks[7], (DEPTH, D_MODEL, D_IN), D_MODEL ** -0.5),
        "conv_dw": nrm(ks[8], (DEPTH, CONV_W, D_CONV), CONV_W ** -0.5),
        "conv_b": nrm(ks[9], (DEPTH, D_CONV), 0.02),
        "ln_g": 1.0 + nrm(ks[10], (DEPTH, D_CONV), 0.02),
        "ln_b": nrm(ks[11], (DEPTH, D_CONV), 0.02),
        "w_conv_out": nrm(ks[12], (DEPTH, D_CONV, D_MODEL), D_CONV ** -0.5),
        "w_pool": nrm(ks[13], (DEPTH, N_POOL_GROUPS, POOL_GC, POOL_GC), POOL_GC ** -0.5),
        "pool_scale": 1.0 + nrm(ks[14], (DEPTH, D_POOL), 0.02),
        "w_out": nrm(ks[15], (DEPTH, D_MODEL, D_MODEL), D_MODEL ** -0.5),
        "norm2_g": 1.0 + nrm(ks[16], (DEPTH, D_MODEL), 0.02),
        "w_up": nrm(ks[17], (DEPTH, D_MODEL, 2 * D_FF), D_MODEL ** -0.5),
        "ffn_dw": nrm(ks[18], (DEPTH, FFN_CONV_W, 2 * D_FF), FFN_CONV_W ** -0.5),
        "w_down": nrm(ks[19], (DEPTH, D_FF, D_MODEL), D_FF ** -0.5),
        "final_g": 1.0 + nrm(ks[20], (D_MODEL,), 0.02),
    }


def reference(x_prompt, x_sample, state_conv, state_pool, state_ffn, meta_tokens,
              norm1_g, w_in, conv_dw, conv_b, ln_g, ln_b, w_conv_out, w_pool, pool_scale,
              w_out, norm2_g, w_up, ffn_dw, w_down, final_g):
    weights = (norm1_g, w_in, conv_dw, conv_b, ln_g, ln_b, w_conv_out, w_pool, pool_scale,
               w_out, norm2_g, w_up, ffn_dw, w_down, final_g)
    dt = x_prompt.dtype
    b = x_prompt.shape[0]
    meta = jnp.broadcast_to(meta_tokens.astype(dt)[None], (b, N_META, D_MODEL))
    xp = jnp.concatenate([meta, x_prompt], axis=1)
    z_conv = jnp.zeros((DEPTH, b, CONV_W - 1, D_CONV), dt)
    z_pool = jnp.zeros((DEPTH, b, POOL_MAX - 1, D_POOL), dt)
    z_ffn = jnp.zeros((DEPTH, b, FFN_CONV_W - 1, 2 * D_FF), dt)
    yp, conv_p, pool_p, ffn_p = run_trunk(xp, z_conv, z_pool, z_ffn, 0, *weights)
    y_prompt = yp[:, N_META:]
    y_sample, conv_s, pool_s, ffn_s = run_trunk(x_sample, state_conv, state_pool, state_ffn,
                                                PAST_LEN, *weights)
    return (y_prompt, y_sample, conv_p, pool_p, ffn_p, conv_s, pool_s, ffn_s)
```

```python
import numpy as np
from contextlib import ExitStack
import concourse.bass as bass
import concourse.mybir as mybir
from concourse.bass_utils import run_bass_kernel_spmd

F32 = mybir.dt.float32
BF16 = mybir.dt.bfloat16
AF = mybir.ActivationFunctionType
ALU = mybir.AluOpType

D = 1024
NCH = 8
SEQ = 2048
NMETA = 16
NSEQ = 16
DSEQ = 8
DEPTH = 2
DFF = 2816
NF = 22
CW = 31
PWIN = (2, 4, 8, 16)
EPS = 1e-6
WCOL = 256
NSLOT = 6
PREF = 3
NDMASEM = 24

V_N1G, V_CB, V_LNG, V_LNB, V_PS, V_N2G, V_FG = 0, 2, 4, 6, 8, 10, 12
NVEC = 13


class Sem:
    def __init__(self, handle, name):
        self.handle = handle
        self.name = name
        self.count = 0


class Eng:
    def __init__(self, name, obj, sem):
        self.name = name
        self.obj = obj
        self.sem = sem
        self.waited = {}


class Buf:
    __slots__ = ("w", "r", "psum")

    def __init__(self, psum=False):
        self.w = None
        self.r = {}
        self.psum = psum


class Seg:
    def __init__(self, kind, plo, n, pc):
        self.kind, self.plo, self.n, self.pc = kind, plo, n, pc


class Tile:
    def __init__(self, idx, lo, n, segs):
        self.idx, self.lo, self.n, self.segs = idx, lo, n, segs


class Group:
    def __init__(self, idx, p0, pw, has_s, first, last):
        self.idx, self.p0, self.pw, self.has_s, self.first, self.last = idx, p0, pw, has_s, first, last
        self.tg = pw + (NSEQ * DSEQ if has_s else 0)
        self.tiles = []
        lo = 0
        ti = 0
        while lo + 512 <= pw:
            self.tiles.append(Tile(ti, lo, 512, [Seg("P", lo, 512, 0)]))
            lo += 512
            ti += 1
        segs = []
        n = 0
        if lo < pw:
            segs.append(Seg("P", lo, pw - lo, 0))
            n = pw - lo
        if has_s:
            segs.append(Seg("S", 0, NSEQ * DSEQ, n))
            n += NSEQ * DSEQ
        if segs:
            self.tiles.append(Tile(ti, lo, n, segs))


class Ext:
    def __init__(self, K, name, H, G, es):
        self.H, self.G = H, G
        self.pwid = H + G.pw
        self.swid = H + DSEQ
        tot = self.pwid + (NSEQ * self.swid if G.has_s else 0)
        self.t = K.sb(name, [128, tot], BF16, es)
        self.B = [Buf() for _ in G.tiles]
        self.Bh = Buf()

    def sview(self):
        return self.t[:, self.pwid:self.pwid + NSEQ * self.swid].rearrange("p (s w) -> p s w", w=self.swid)

    def dst(self, seg):
        if seg.kind == "P":
            return self.t[:, self.H + seg.plo:self.H + seg.plo + seg.n]
        return self.sview()[:, :, self.H:self.H + DSEQ]

    def rhs(self, seg, back):
        if seg.kind == "P":
            a = self.H + seg.plo - back
            return self.t[:, a:a + seg.n]
        return self.sview()[:, :, self.H - back:self.H - back + DSEQ]

    def deps(self, t):
        d = [self.B[t.idx], self.Bh]
        if t.idx > 0:
            d.append(self.B[t.idx - 1])
        return d


class CT:
    def __init__(self, K, name, nch, width, dt, es):
        self.ts = [K.sb(f"{name}{c}", [128, width], dt, es) for c in range(nch)]

    def __getitem__(self, key):
        p, c, sl = key
        return self.ts[c][:, sl]


def pview(bank, seg):
    if seg.kind == "P":
        return bank[:, seg.pc:seg.pc + seg.n]
    return bank[:, seg.pc:seg.pc + seg.n].rearrange("p (s t) -> p s t", t=DSEQ)


class K:
    def __init__(self):
        self.nc = bass.Bass("TRN2", target_bir_lowering=False)
        self.es = ExitStack()
        self.uid = 0

    def sb(self, name, shape, dt, es=None):
        self.uid += 1
        return (es or self.es).enter_context(self.nc.sbuf_tensor(f"{name}_{self.uid}", shape, dt))

    def mksem(self, name):
        return Sem(self.es.enter_context(self.nc.semaphore(name)), name)

    def wait(self, eng, toks):
        best = {}
        for s, v in toks:
            if best.get(s, 0) < v:
                best[s] = v
        for s, v in best.items():
            if eng.waited.get(s, 0) < v:
                eng.obj.wait_ge(s.handle, v)
                eng.waited[s] = v

    def _deps(self, eng, reads, writes):
        deps = []
        for b in reads:
            if b.w is not None:
                deps.append(b.w)
            if b.psum and eng is not self.PE:
                for s, v in b.r.items():
                    if s is not eng.sem and s is not self.PE.sem:
                        deps.append((s, v))
        skip_own = eng is self.PE
        for b in writes:
            if b.w is not None and not (skip_own and b.w[0] is eng.sem):
                deps.append(b.w)
            for s, v in b.r.items():
                if not (skip_own and s is eng.sem):
                    deps.append((s, v))
        return deps

    def emit(self, eng, fn, reads=(), writes=()):
        self.wait(eng, self._deps(eng, reads, writes))
        ins = fn()
        eng.sem.count += 1
        ins.then_inc(eng.sem.handle, 1)
        tok = (eng.sem, eng.sem.count)
        for b in writes:
            b.w = tok
            b.r = {}
        for b in reads:
            b.r[eng.sem] = tok[1]
        return tok

    def pe_group(self, mms, reads, writes):
        eng = self.PE
        self.wait(eng, self._deps(eng, reads, writes))
        ins = None
        for m in mms:
            ins = self.nc.tensor.matmul(m["out"], lhsT=m["lhsT"], rhs=m["rhs"], start=m["start"], stop=m["stop"])
        eng.sem.count += 1
        ins.then_inc(eng.sem.handle, 1)
        tok = (eng.sem, eng.sem.count)
        for b in writes:
            b.w = tok
            b.r = {}
        for b in reads:
            b.r[eng.sem] = tok[1]
        return tok

    def pe_transposes(self, trs, reads, writes):
        eng = self.PE
        self.wait(eng, self._deps(eng, reads, writes))
        ins = None
        for (o, i, idn) in trs:
            ins = self.nc.tensor.transpose(o, i, idn)
        eng.sem.count += 1
        ins.then_inc(eng.sem.handle, 1)
        tok = (eng.sem, eng.sem.count)
        for b in writes:
            b.w = tok
            b.r = {}
        for b in reads:
            b.r[eng.sem] = tok[1]
        return tok

    def dma(self, eng, out, in_, reads=(), writes=()):
        if eng is self.POOL:
            s = self.wsems[self.wsem_i % len(self.wsems)]
            self.wsem_i += 1
        else:
            s = self.dsems[self.dsem_i % len(self.dsems)]
            self.dsem_i += 1
        deps = self._deps(eng, reads, writes)
        if s.count > 0:
            deps.append((s, s.count))
        self.wait(eng, deps)
        ins = eng.obj.dma_start(out=out, in_=in_)
        s.count += 16
        ins.then_inc(s.handle, 16)
        tok = (s, s.count)
        for b in writes:
            b.w = tok
            b.r = {}
        for b in reads:
            b.r[s] = tok[1]
        return tok

    def bank(self):
        while True:
            i = self.bank_i % 8
            self.bank_i += 1
            if i not in self.reserved:
                return self.banks[i], self.bankB[i]

    def drain(self, n):
        while n > 0 and self.deferq:
            self.deferq.popleft()()
            n -= 1

    def plan_weights(self, ngroups):
        a = self.ap
        plan = []
        for g in range(ngroups):
            for l in range(DEPTH):
                win, wco, wp, wo, wup, wdn = a["w_in"], a["w_conv_out"], a["w_pool"], a["w_out"], a["w_up"], a["w_down"]
                for q in range(4):
                    plan.append((("za", g, l, q), win[l, :, 256 * q:256 * (q + 1)], 8))
                    plan.append((("zg", g, l, q), win[l, :, 1024 + 256 * q:1024 + 256 * (q + 1)], 8))
                for q in range(4):
                    plan.append((("zp", g, l, q), win[l, :, 2048 + 256 * q:2048 + 256 * (q + 1)], 8))
                    plan.append((("wp", g, l, q), wp[l, q, :, :], 2))
                    plan.append((("gb", g, l, q), win[l, :, 4096 + 256 * q:4096 + 256 * (q + 1)], 8))
                for q in range(4):
                    plan.append((("wco", g, l, q), wco[l, :, 256 * q:256 * (q + 1)], 8))
                    plan.append((("ga", g, l, q), win[l, :, 3072 + 256 * q:3072 + 256 * (q + 1)], 8))
                for q in range(4):
                    plan.append((("wout", g, l, q), wo[l, :, 256 * q:256 * (q + 1)], 8))
                for q in range(11):
                    plan.append((("upv", g, l, q), wup[l, :, 256 * q:256 * (q + 1)], 8))
                    plan.append((("upg", g, l, q), wup[l, :, DFF + 256 * q:DFF + 256 * (q + 1)], 8))
                for q in range(4):
                    for r in range(3):
                        r0 = 1024 * r
                        r1 = min(DFF, r0 + 1024)
                        plan.append((("wd", g, l, q, r), wdn[l, r0:r1, 256 * q:256 * (q + 1)], (r1 - r0) // 128))
        self.plan = plan
        self.w_issued = 0
        self.w_taken = 0
        self.w_done = [False] * len(plan)

    def w_issue_upto(self, idx):
        idx = min(idx, len(self.plan) - 1)
        while self.w_issued <= idx:
            i = self.w_issued
            tag, src, nkc = self.plan[i]
            if i >= NSLOT:
                assert self.w_done[i - NSLOT], (i, tag)
            s = i % NSLOT
            self.dma(self.POOL, out=self.wslot[s][:, 0:nkc, :], in_=src.rearrange("(c p) n -> p c n", p=128),
                     writes=[self.wslotB[s]])
            self.w_issued += 1

    def w_take(self, k, tags):
        first = self.w_taken
        res = []
        for j in range(k):
            i = first + j
            assert self.plan[i][0][0] == tags[j], (self.plan[i][0], tags)
            res.append(i)
        self.w_taken += k
        self.w_issue_upto(first + k - 1 + PREF)
        return res

    def w_slot(self, i):
        return self.wslot[i % NSLOT], self.wslotB[i % NSLOT]

    def w_retire(self, idxs):
        for i in idxs:
            self.w_done[i] = True

    def rows_in(self, src, R, nch, dst_fn):
        nc = self.nc
        srcs = src if isinstance(src, list) else [(src, 0, R)]
        for c0 in range(0, nch, 8):
            ncc = min(8, nch - c0)
            i = self.st_i % 3
            self.st_i += 1
            st, stB = self.stage[i], self.stageB[i]
            qe = self.SP
            self.wait(qe, self._deps(qe, (), [stB]))
            pBs = []
            for (sap, rlo, rn) in srcs:
                pB = Buf()
                self.dma(qe, out=st[rlo:rlo + rn, 0:ncc * 128], in_=sap[:, c0 * 128:(c0 + ncc) * 128], writes=[pB])
                pBs.append(pB)
            for q in range(0, ncc, 4):
                nk = min(4, ncc - q)
                bk, bB = self.bank()
                trs = [(bk[:, k * 128:k * 128 + R], st[:R, (q + k) * 128:(q + k + 1) * 128], self.ident[:R, :R]) for k in range(nk)]
                self.pe_transposes(trs, reads=[stB, self.identB] + pBs, writes=[bB])
                for k in range(nk):
                    dst, dB = dst_fn(c0 + q + k)
                    if (self.alt % 2) == 0:
                        self.emit(self.ACT, lambda: nc.scalar.activation(out=dst, in_=bk[:, k * 128:k * 128 + R], func=AF.Copy),
                                  reads=[bB], writes=[dB])
                    else:
                        self.emit(self.DVE, lambda: nc.vector.tensor_copy(out=dst, in_=bk[:, k * 128:k * 128 + R]),
                                  reads=[bB], writes=[dB])
                self.alt += 1

    def rows_out(self, src_fn, R, nch, dests):
        nc = self.nc
        for c0 in range(0, nch, 8):
            ncc = min(8, nch - c0)
            i = self.st_i % 3
            self.st_i += 1
            st, stB = self.stage[i], self.stageB[i]
            for q in range(0, ncc, 4):
                nk = min(4, ncc - q)
                bk, bB = self.bank()
                trs = []
                rd = [self.identB]
                for k in range(nk):
                    s, sB = src_fn(c0 + q + k)
                    trs.append((bk[:R, k * 128:(k + 1) * 128], s, self.ident[:, :]))
                    rd.append(sB)
                self.pe_transposes(trs, reads=rd, writes=[bB])
                if (self.alt % 2) == 0:
                    self.emit(self.ACT, lambda: nc.scalar.activation(out=st[:R, q * 128:(q + nk) * 128], in_=bk[:R, 0:nk * 128], func=AF.Copy),
                              reads=[bB], writes=[stB])
                else:
                    self.emit(self.DVE, lambda: nc.vector.tensor_copy(out=st[:R, q * 128:(q + nk) * 128], in_=bk[:R, 0:nk * 128]),
                              reads=[bB], writes=[stB])
                self.alt += 1
            self.wait(self.SP, [(self.ACT.sem, self.ACT.sem.count), (self.DVE.sem, self.DVE.sem.count)])
            for (rlo, rn, fn) in dests:
                self.dma(self.SP, out=fn(c0, ncc), in_=st[rlo:rlo + rn, 0:ncc * 128], reads=[stB])

    def mm8(self, bank, slot, cc, src, t, nkc=8):
        return [dict(out=bank[:, 0:t.n], lhsT=slot[:, kc, cc * 128:(cc + 1) * 128], rhs=src[:, kc, t.lo:t.lo + t.n],
                     start=(kc == 0), stop=(kc == nkc - 1)) for kc in range(nkc)]

    def tail_pieces(self, G, t, H):
        res = []
        if not G.last:
            return res
        for seg in t.segs:
            if seg.kind == "P":
                a = max(seg.plo, G.pw - H)
                b = seg.plo + seg.n
                if b > a:
                    res.append(("P", seg.pc + a - seg.plo, b - a, a - (G.pw - H)))
            else:
                res.append(("S", seg.pc, seg.n, H))
        return res

    def norm(self, G, x, xB, h, hB, vidx, tmp):
        nc = self.nc
        for t in G.tiles:
            n = t.n
            bS, bSB = self.bank()
            for c in range(NCH):
                sq, sqB = tmp["sq"][self.sq_i % 3], tmp["sqB"][self.sq_i % 3]
                self.sq_i += 1
                self.emit(self.ACT, lambda: nc.scalar.activation(out=sq[:, 0:n], in_=x[:, c, t.lo:t.lo + n], func=AF.Square),
                          reads=[xB[c][t.idx]], writes=[sqB])
                self.pe_group([dict(out=bS[:, 0:n], lhsT=self.onesD[:, :], rhs=sq[:, 0:n], start=(c == 0), stop=(c == NCH - 1))],
                              reads=[sqB, self.constB], writes=[bSB])
            sd, sdB = tmp["f"][0], tmp["fB"][0]
            rs, rsB = tmp["f"][1], tmp["fB"][1]
            self.emit(self.ACT, lambda: nc.scalar.activation(out=sd[:, 0:n], in_=bS[:, 0:n], func=AF.Sqrt, bias=self.epsc[:, 0:1]),
                      reads=[bSB, self.constB], writes=[sdB])
            self.emit(self.DVE, lambda: nc.vector.reciprocal(out=rs[:, 0:n], in_=sd[:, 0:n]), reads=[sdB], writes=[rsB])
            for c in range(NCH):
                self.emit(self.DVE, lambda: nc.vector.scalar_tensor_tensor(
                    out=h[:, c, t.lo:t.lo + n], in0=x[:, c, t.lo:t.lo + n], scalar=self.vecs[:, c, vidx:vidx + 1],
                    in1=rs[:, 0:n], op0=ALU.mult, op1=ALU.mult), reads=[xB[c][t.idx], rsB, self.vecB], writes=[hB[c][t.idx]])

    def fill_hist(self, G, ext, l, c, hist, sstate):
        nc = self.nc
        H = ext.H
        if G.first:
            self.emit(self.DVE, lambda: nc.vector.memset(ext.t[:, 0:H], 0.0), writes=[ext.Bh])
        else:
            self.emit(self.DVE, lambda: nc.vector.tensor_copy(out=ext.t[:, 0:H], in_=hist[l][:, c, :]),
                      reads=[self.histB], writes=[ext.Bh])
        if G.has_s:
            self.emit(self.DVE, lambda: nc.vector.tensor_copy(out=ext.sview()[:, :, 0:H], in_=sstate[:, c, :].rearrange("p (s r) -> p s r", r=H)),
                      reads=[self.sstB], writes=[ext.Bh])

    def save_hist(self, G, ext, l, c, hist):
        nc = self.nc
        H = ext.H
        if G.last:
            return
        rd = [ext.B[t.idx] for t in G.tiles]
        self.emit(self.DVE, lambda: nc.vector.tensor_copy(out=hist[l][:, c, :], in_=ext.t[:, G.pw:G.pw + H]),
                  reads=rd, writes=[self.histB])

    def mixer(self, G, l, x, xB, h, hB):
        nc = self.nc
        with ExitStack() as es:
            TG = G.tg
            cs = CT(self, "cs", NCH, TG, BF16, es)
            pm = CT(self, "pm", NCH, TG, BF16, es)
            csB = [[Buf() for _ in G.tiles] for _ in range(NCH)]
            pmB = [[Buf() for _ in G.tiles] for _ in range(NCH)]
            zext = [Ext(self, "zext", 15, G, es) for _ in range(2)]
            aext_es = None
            pbuf = CT(self, "pbuf", 2, TG, BF16, es)
            pbufB = [[Buf() for _ in G.tiles] for _ in range(2)]
            tmp = dict(
                sq=[self.sb("sq", [128, 512], BF16, es) for _ in range(3)], sqB=[Buf() for _ in range(3)],
                f=[self.sb("ft", [128, 512], F32, es) for _ in range(4)], fB=[Buf() for _ in range(4)],
            )
            sg = [self.sb("sg", [128, 512], F32, es) for _ in range(3)]
            sgB = [Buf() for _ in range(3)]
            self.sgi = 0
            lnA = [self.sb("lnA", [128, t_.n], F32, es) for t_ in G.tiles]
            lnB = [self.sb("lnB", [128, t_.n], F32, es) for t_ in G.tiles]
            lnAB = [Buf() for _ in G.tiles]
            lnBB = [Buf() for _ in G.tiles]
            et = [self.sb("et", [128, 16], F32, es) for _ in range(2)]
            etB = [Buf(), Buf()]
            sconv = spool = tailA = tailP = None
            tailAB, tailPB = Buf(), Buf()
            self.sstB = Buf()
            if G.has_s:
                sconv = self.sb("sconv", [128, NCH, NSEQ * 30], BF16, es)
                spool = self.sb("spool", [128, NCH, NSEQ * 15], BF16, es)
                tailA = self.sb("tailA", [128, NCH, 30 + 128], F32, es)
                tailP = self.sb("tailP", [128, NCH, 15 + 128], F32, es)
                sc = self.ap["state_conv"]
                sp = self.ap["state_pool"]
                for r0 in range(0, NSEQ * 30, 120):
                    s0 = r0 // 30
                    self.rows_in(sc[l, s0:s0 + 4, :, :].rearrange("s r d -> (s r) d"), 120, NCH,
                                 lambda k, s0=s0: (sconv[:, k, s0 * 30:(s0 + 4) * 30], self.sstB))
                for r0 in range(0, NSEQ * 15, 120):
                    s0 = r0 // 15
                    self.rows_in(sp[l, s0:s0 + 8, :, :].rearrange("s r d -> (s r) d"), 120, NCH,
                                 lambda k, s0=s0: (spool[:, k, s0 * 15:(s0 + 8) * 15], self.sstB))

            self.norm(G, x, xB, h, hB, V_N1G + l, tmp)

            ces = ExitStack()
            aext = [Ext(self, "aext", 30, G, ces) for _ in range(2)]
            d31 = self.sb("d31", [128, CW, 128], BF16, ces)
            d31B = Buf()
            for q in range(4):
                ia, ig = self.w_take(2, ["za", "zg"])
                (sa, saB), (sgt, sgtB) = self.w_slot(ia), self.w_slot(ig)
                for cc in range(2):
                    c = 2 * q + cc
                    ae = aext[c % 2]
                    self.emit(self.DVE, lambda: nc.vector.tensor_tensor(
                        out=d31[:], in0=self.identb[:].unsqueeze(1).to_broadcast([128, CW, 128]),
                        in1=self.cw[l][:, c, :].unsqueeze(2).to_broadcast([128, CW, 128]), op=ALU.mult),
                        reads=[self.constB, self.vecB], writes=[d31B])
                    self.fill_hist(G, ae, l, c, self.hista, sconv)
                    for t in G.tiles:
                        n = t.n
                        bA, bAB = self.bank()
                        self.pe_group(self.mm8(bA, sa, cc, h, t), reads=[saB] + [hB[kc][t.idx] for kc in range(NCH)], writes=[bAB])
                        bG, bGB = self.bank()
                        self.pe_group(self.mm8(bG, sgt, cc, h, t), reads=[sgtB] + [hB[kc][t.idx] for kc in range(NCH)], writes=[bGB])
                        s_, s_B = sg[self.sgi % 3], sgB[self.sgi % 3]
                        self.sgi += 1
                        self.emit(self.ACT, lambda: nc.scalar.activation(out=s_[:, 0:n], in_=bG[:, 0:n], func=AF.Sigmoid), reads=[bGB], writes=[s_B])
                        for seg in t.segs:
                            self.emit(self.DVE, lambda: nc.vector.tensor_tensor(out=ae.dst(seg), in0=pview(bA, seg), in1=pview(s_, seg), op=ALU.mult),
                                      reads=[bAB, s_B], writes=[ae.B[t.idx]])
                        for (kind, pl, m, tl) in self.tail_pieces(G, t, 30):
                            self.emit(self.DVE, lambda: nc.vector.tensor_tensor(out=tailA[:, c, tl:tl + m], in0=bA[:, pl:pl + m], in1=s_[:, pl:pl + m], op=ALU.mult),
                                      reads=[bAB, s_B], writes=[tailAB])
                    self.save_hist(G, ae, l, c, self.hista)
                    for t in G.tiles:
                        bC, bCB = self.bank()
                        mms = []
                        for seg in t.segs:
                            for k in range(CW):
                                mms.append(dict(out=pview(bC, seg), lhsT=d31[:, k, :], rhs=ae.rhs(seg, CW - 1 - k),
                                                start=(k == 0), stop=(k == CW - 1)))
                        self.pe_group(mms, reads=ae.deps(t) + [d31B], writes=[bCB])
                        self.emit(self.ACT, lambda: nc.scalar.activation(out=cs[:, c, t.lo:t.lo + t.n], in_=bC[:, 0:t.n], func=AF.Identity,
                                                                         bias=self.vecs[:, c, V_CB + l:V_CB + l + 1]),
                                  reads=[bCB, self.vecB], writes=[csB[c][t.idx]])
                self.w_retire([ia, ig])

            ces.close()
            lsg = [self.sb("lsg", [128, 512], F32, es) for _ in range(3)]
            lsgB = [Buf() for _ in range(3)]
            lz = [self.sb("lz", [128, 512], F32, es) for _ in range(3)]
            lzB = [Buf() for _ in range(3)]
            self.wait(self.ACT, [(self.PE.sem, self.PE.sem.count), (self.DVE.sem, self.DVE.sem.count)])
            self.reserved = {6, 7}
            bM, bMB = self.banks[6], self.bankB[6]
            bQ, bQB = self.banks[7], self.bankB[7]
            f, fB = tmp["f"], tmp["fB"]

            def s1(t, c, k):
                n = t.n
                sq, sqB = tmp["sq"][k % 3], tmp["sqB"][k % 3]
                self.emit(self.ACT, lambda: nc.scalar.activation(out=sq[:, 0:n], in_=cs[:, c, t.lo:t.lo + n], func=AF.Square),
                          reads=[csB[c][t.idx]], writes=[sqB])

            def s2(t, c, k):
                n = t.n
                sq, sqB = tmp["sq"][k % 3], tmp["sqB"][k % 3]
                self.pe_group([dict(out=bM[:, 0:n], lhsT=self.onesD[:, :], rhs=cs[:, c, t.lo:t.lo + n], start=(c == 0), stop=(c == NCH - 1))],
                              reads=[csB[c][t.idx], self.constB], writes=[bMB])
                self.pe_group([dict(out=bQ[:, 0:n], lhsT=self.onesD[:, :], rhs=sq[:, 0:n], start=(c == 0), stop=(c == NCH - 1))],
                              reads=[sqB, self.constB], writes=[bQB])

            def u_ab(t):
                n = t.n
                A, AB_ = lnA[t.idx], lnAB[t.idx]
                Bm, BmB = lnB[t.idx], lnBB[t.idx]
                self.emit(self.ACT, lambda: nc.scalar.activation(out=f[0][:, 0:n], in_=bM[:, 0:n], func=AF.Square), reads=[bMB], writes=[fB[0]])
                self.emit(self.DVE, lambda: nc.vector.tensor_tensor(out=f[1][:, 0:n], in0=bQ[:, 0:n], in1=f[0][:, 0:n], op=ALU.subtract),
                          reads=[bQB, fB[0]], writes=[fB[1]])
                self.emit(self.ACT, lambda: nc.scalar.activation(out=f[0][:, 0:n], in_=f[1][:, 0:n], func=AF.Sqrt, bias=self.epsc[:, 0:1]),
                          reads=[fB[1], self.constB], writes=[fB[0]])
                self.emit(self.DVE, lambda: nc.vector.reciprocal(out=A[:, 0:n], in_=f[0][:, 0:n]), reads=[fB[0]], writes=[AB_])
                self.emit(self.DVE, lambda: nc.vector.scalar_tensor_tensor(out=Bm[:, 0:n], in0=bM[:, 0:n], scalar=-1.0, in1=A[:, 0:n],
                                                                           op0=ALU.mult, op1=ALU.mult), reads=[bMB, AB_], writes=[BmB])

            def a1(t, c, k):
                n = t.n
                A, AB_ = lnA[t.idx], lnAB[t.idx]
                Bm, BmB = lnB[t.idx], lnBB[t.idx]
                u, uB = f[1 + (k % 3)], fB[1 + (k % 3)]
                s_, s_B = lsg[k % 3], lsgB[k % 3]
                z_, z_B = lz[k % 3], lzB[k % 3]
                gcol = self.vecs[:, c, V_LNG + l:V_LNG + l + 1]
                bcol = self.vecs[:, c, V_LNB + l:V_LNB + l + 1]
                self.emit(self.DVE, lambda: nc.vector.tensor_tensor(out=u[:, 0:n], in0=cs[:, c, t.lo:t.lo + n], in1=A[:, 0:n], op=ALU.mult),
                          reads=[csB[c][t.idx], AB_], writes=[uB])
                self.emit(self.DVE, lambda: nc.vector.tensor_tensor(out=u[:, 0:n], in0=u[:, 0:n], in1=Bm[:, 0:n], op=ALU.add),
                          reads=[uB, BmB], writes=[uB])
                self.emit(self.ACT, lambda: nc.scalar.activation(out=s_[:, 0:n], in_=u[:, 0:n], func=AF.Sigmoid, scale=gcol, bias=bcol),
                          reads=[uB, self.vecB], writes=[s_B])
                self.emit(self.ACT, lambda: nc.scalar.activation(out=z_[:, 0:n], in_=u[:, 0:n], func=AF.Identity, scale=gcol, bias=bcol),
                          reads=[uB, self.vecB], writes=[z_B])

            def a2(t, c, k):
                n = t.n
                s_, s_B = lsg[k % 3], lsgB[k % 3]
                z_, z_B = lz[k % 3], lzB[k % 3]
                self.emit(self.DVE, lambda: nc.vector.tensor_tensor(out=cs[:, c, t.lo:t.lo + n], in0=z_[:, 0:n], in1=s_[:, 0:n], op=ALU.mult),
                          reads=[s_B, z_B], writes=[csB[c][t.idx]])

            def lagged(firsts, seconds, lag):
                out = []
                nU = len(firsts)
                for i in range(nU + lag):
                    if i < nU:
                        out.append(firsts[i])
                    if i - lag >= 0:
                        out.append(seconds[i - lag])
                return out

            kk = 0
            for t in G.tiles:
                F1, F2 = [], []
                for c in range(NCH):
                    F1.append(lambda t=t, c=c, k=kk: s1(t, c, k))
                    F2.append(lambda t=t, c=c, k=kk: s2(t, c, k))
                    kk += 1
                self.deferq.extend(lagged(F1, F2, 2))
                self.deferq.append(lambda t=t: u_ab(t))
            F1, F2 = [], []
            kk = 0
            for t in G.tiles:
                for c in range(NCH):
                    F1.append(lambda t=t, c=c, k=kk: a1(t, c, k))
                    F2.append(lambda t=t, c=c, k=kk: a2(t, c, k))
                    kk += 1
            self.deferq.extend(lagged(F1, F2, 2))

            for g in range(4):
                win = PWIN[g]
                iz, iw, ig = self.w_take(3, ["zp", "wp", "gb"])
                (sz, szB), (sw, swB), (sgb, sgbB) = self.w_slot(iz), self.w_slot(iw), self.w_slot(ig)
                for cc in range(2):
                    c = 2 * g + cc
                    ze = zext[cc]
                    self.fill_hist(G, ze, l, c, self.histz, spool)
                    for t in G.tiles:
                        bk, bB = self.bank()
                        self.pe_group(self.mm8(bk, sz, cc, h, t), reads=[szB] + [hB[kc][t.idx] for kc in range(NCH)], writes=[bB])
                        self.drain(2)
                        for seg in t.segs:
                            self.emit(self.ACT, lambda: nc.scalar.activation(out=ze.dst(seg), in_=pview(bk, seg), func=AF.Copy),
                                      reads=[bB], writes=[ze.B[t.idx]])
                        for (kind, pl, m, tl) in self.tail_pieces(G, t, 15):
                            self.emit(self.DVE, lambda: nc.vector.tensor_copy(out=tailP[:, c, tl:tl + m], in_=bk[:, pl:pl + m]),
                                      reads=[bB], writes=[tailPB])
                    self.save_hist(G, ze, l, c, self.histz)
                for cc in range(2):
                    ze = zext[cc]
                    for t in G.tiles:
                        bk, bB = self.bank()
                        mms = []
                        for seg in t.segs:
                            for j in range(win):
                                mms.append(dict(out=pview(bk, seg), lhsT=self.pd[g][0 if j == 0 else 1][:, :], rhs=ze.rhs(seg, j),
                                                start=(j == 0), stop=(j == win - 1)))
                        self.pe_group(mms, reads=ze.deps(t) + [self.constB], writes=[bB])
                        self.drain(2)
                        self.emit(self.ACT, lambda: nc.scalar.activation(out=pbuf[:, cc, t.lo:t.lo + t.n], in_=bk[:, 0:t.n], func=AF.Copy),
                                  reads=[bB], writes=[pbufB[cc][t.idx]])
                        if G.first and t.idx == 0:
                            self.emit(self.DVE, lambda: nc.vector.tensor_tensor(out=et[0][:, 0:15], in0=ze.t[:, 15:30], in1=self.rm1[g][:, 0:15], op=ALU.mult),
                                      reads=ze.deps(t) + [self.constB], writes=[etB[0]])
                            self.emit(self.DVE, lambda: nc.vector.tensor_tensor(out=et[1][:, 0:15], in0=bk[:, 0:15], in1=self.rr[g][:, 0:15], op=ALU.mult),
                                      reads=[bB, self.constB], writes=[etB[1]])
                            self.emit(self.DVE, lambda: nc.vector.tensor_tensor(out=pbuf[:, cc, 0:15], in0=et[0][:, 0:15], in1=et[1][:, 0:15], op=ALU.add),
                                      reads=[etB[0], etB[1]], writes=[pbufB[cc][t.idx]])
                for oc in range(2):
                    d = 2 * g + oc
                    for t in G.tiles:
                        n = t.n
                        bP, bPB = self.bank()
                        self.pe_group(self.mm8(bP, sw, oc, pbuf, t, nkc=2), reads=[swB, pbufB[0][t.idx], pbufB[1][t.idx]], writes=[bPB])
                        bG, bGB = self.bank()
                        self.pe_group(self.mm8(bG, sgb, oc, h, t), reads=[sgbB] + [hB[kc][t.idx] for kc in range(NCH)], writes=[bGB])
                        self.drain(2)
                        s_, s_B = sg[self.sgi % 3], sgB[self.sgi % 3]
                        self.sgi += 1
                        self.emit(self.ACT, lambda: nc.scalar.activation(out=s_[:, 0:n], in_=bG[:, 0:n], func=AF.Sigmoid), reads=[bGB], writes=[s_B])
                        self.emit(self.DVE, lambda: nc.vector.scalar_tensor_tensor(
                            out=pm[:, d, t.lo:t.lo + n], in0=bP[:, 0:n], scalar=self.vecs[:, d, V_PS + l:V_PS + l + 1], in1=s_[:, 0:n],
                            op0=ALU.mult, op1=ALU.mult), reads=[bPB, s_B, self.vecB], writes=[pmB[d][t.idx]])
                self.w_retire([iz, iw, ig])

            self.drain(10 ** 6)
            self.reserved = set()

            for q in range(4):
                io, ig = self.w_take(2, ["wco", "ga"])
                (so, soB), (sga, sgaB) = self.w_slot(io), self.w_slot(ig)
                for cc in range(2):
                    d = 2 * q + cc
                    for t in G.tiles:
                        n = t.n
                        bO, bOB = self.bank()
                        self.pe_group(self.mm8(bO, so, cc, cs, t), reads=[soB] + [csB[kc][t.idx] for kc in range(NCH)], writes=[bOB])
                        bG, bGB = self.bank()
                        self.pe_group(self.mm8(bG, sga, cc, h, t), reads=[sgaB] + [hB[kc][t.idx] for kc in range(NCH)], writes=[bGB])
                        s_, s_B = sg[self.sgi % 3], sgB[self.sgi % 3]
                        self.sgi += 1
                        self.emit(self.ACT, lambda: nc.scalar.activation(out=s_[:, 0:n], in_=bG[:, 0:n], func=AF.Sigmoid), reads=[bGB], writes=[s_B])
                        self.emit(self.DVE, lambda: nc.vector.tensor_tensor(out=s_[:, 0:n], in0=bO[:, 0:n], in1=s_[:, 0:n], op=ALU.mult),
                                  reads=[bOB, s_B], writes=[s_B])
                        self.emit(self.DVE, lambda: nc.vector.tensor_tensor(out=pm[:, d, t.lo:t.lo + n], in0=s_[:, 0:n], in1=pm[:, d, t.lo:t.lo + n], op=ALU.add),
                                  reads=[s_B, pmB[d][t.idx]], writes=[pmB[d][t.idx]])
                self.w_retire([io, ig])

            for q in range(4):
                (io,) = self.w_take(1, ["wout"])
                so, soB = self.w_slot(io)
                for cc in range(2):
                    d = 2 * q + cc
                    for t in G.tiles:
                        n = t.n
                        bO, bOB = self.bank()
                        self.pe_group(self.mm8(bO, so, cc, pm, t), reads=[soB] + [pmB[kc][t.idx] for kc in range(NCH)], writes=[bOB])
                        self.emit(self.DVE, lambda: nc.vector.tensor_tensor(out=x[:, d, t.lo:t.lo + n], in0=bO[:, 0:n], in1=x[:, d, t.lo:t.lo + n], op=ALU.add),
                                  reads=[bOB, xB[d][t.idx]], writes=[xB[d][t.idx]])
                self.w_retire([io])

            if G.last:
                o = self.ap
                b30 = lambda c0, ncc: o["ncp"][l, :, c0 * 128:(c0 + ncc) * 128]
                self.rows_out(lambda k: (tailA[:, k, 0:30], tailAB), 30, NCH, [(0, 30, b30)])
                dests = [(s * DSEQ, DSEQ, (lambda c0, ncc, s=s: o["ncs"][l, s, 22:30, c0 * 128:(c0 + ncc) * 128])) for s in range(NSEQ)]
                self.rows_out(lambda k: (tailA[:, k, 30:158], tailAB), 128, NCH, dests)
                b15 = lambda c0, ncc: o["npp"][l, :, c0 * 128:(c0 + ncc) * 128]
                self.rows_out(lambda k: (tailP[:, k, 0:15], tailPB), 15, NCH, [(0, 15, b15)])
                dests = [(s * DSEQ, DSEQ, (lambda c0, ncc, s=s: o["nps"][l, s, 7:15, c0 * 128:(c0 + ncc) * 128])) for s in range(NSEQ)]
                self.rows_out(lambda k: (tailP[:, k, 15:143], tailPB), 128, NCH, dests)
            self.barrier()

    def ffn(self, G, l, x, xB, h, hB):
        nc = self.nc
        with ExitStack() as es:
            TG = G.tg
            act = CT(self, "act", NF, TG, BF16, es)
            actB = [[Buf() for _ in G.tiles] for _ in range(NF)]
            vext = [Ext(self, "vext", 2, G, es) for _ in range(2)]
            gext = [Ext(self, "gext", 2, G, es) for _ in range(2)]
            acc = [self.sb("acc", [128, 512], F32, es) for _ in range(4)]
            accB = [Buf() for _ in range(4)]
            acci = 0
            d3 = [self.sb("d3", [128, 3, 128], BF16, es) for _ in range(2)]
            d3B = [Buf(), Buf()]
            self.gli = 0
            tmp = dict(
                sq=[self.sb("sq", [128, 512], BF16, es) for _ in range(3)], sqB=[Buf() for _ in range(3)],
                f=[self.sb("ft", [128, 512], F32, es) for _ in range(2)], fB=[Buf() for _ in range(2)],
            )
            gl = [self.sb("gl", [128, 512], F32, es) for _ in range(3)]
            glB = [Buf() for _ in range(3)]
            gli = 0
            sffn = tailF = None
            tailFB = Buf()
            self.sstB = Buf()
            if G.has_s:
                sffn = self.sb("sffn", [128, 2 * NF, NSEQ * 2], BF16, es)
                tailF = self.sb("tailF", [128, 2 * NF, 2 + 2 * NSEQ], F32, es)
                sf = self.ap["state_ffn"]
                self.rows_in(sf[l, :, :, :].rearrange("s r d -> (s r) d"), 2 * NSEQ, 2 * NF,
                             lambda k: (sffn[:, k, :], self.sstB))

            self.norm(G, x, xB, h, hB, V_N2G + l, tmp)

            pending = None
            for q in range(11):
                iv, ig = self.w_take(2, ["upv", "upg"])
                (sv, svB), (sgg, sggB) = self.w_slot(iv), self.w_slot(ig)
                for cc in range(2):
                    f = 2 * q + cc
                    ve, ge = vext[f % 2], gext[f % 2]
                    dd, ddB = d3[f % 2], d3B[f % 2]
                    self.emit(self.DVE, lambda: nc.vector.tensor_tensor(
                        out=dd[:], in0=self.identb[:].unsqueeze(1).to_broadcast([128, 3, 128]),
                        in1=self.fw[l][:, NF + f, :].unsqueeze(2).to_broadcast([128, 3, 128]), op=ALU.mult),
                        reads=[self.constB, self.vecB], writes=[ddB])
                    self.fill_hist(G, ve, l, f, self.histf, sffn)
                    self.fill_hist(G, ge, l, NF + f, self.histf, sffn)
                    for t in G.tiles:
                        n = t.n
                        bV, bVB = self.bank()
                        self.pe_group(self.mm8(bV, sv, cc, h, t), reads=[svB] + [hB[kc][t.idx] for kc in range(NCH)], writes=[bVB])
                        bG, bGB = self.bank()
                        self.pe_group(self.mm8(bG, sgg, cc, h, t), reads=[sggB] + [hB[kc][t.idx] for kc in range(NCH)], writes=[bGB])
                        aV, aVB = acc[acci % 4], accB[acci % 4]
                        acci += 1
                        for (ext, bk, bB, fc) in ((ve, bV, bVB, f), (ge, bG, bGB, NF + f)):
                            for seg in t.segs:
                                self.emit(self.ACT, lambda: nc.scalar.activation(out=ext.dst(seg), in_=pview(bk, seg), func=AF.Copy),
                                          reads=[bB], writes=[ext.B[t.idx]])
                            if ext is ve:
                                self.emit(self.ACT, lambda: nc.scalar.activation(out=aV[:, 0:n], in_=bk[:, 0:n], func=AF.Identity,
                                                                                 scale=self.fw[l][:, fc, 2:3]),
                                          reads=[bB, self.vecB], writes=[aVB])
                            for (kind, pl, m, tl) in self.tail_pieces(G, t, 2):
                                if kind == "P":
                                    self.emit(self.ACT, lambda: nc.scalar.activation(out=tailF[:, fc, tl:tl + m], in_=bk[:, pl:pl + m], func=AF.Copy),
                                              reads=[bB], writes=[tailFB])
                                else:
                                    src = bk[:, pl:pl + m].rearrange("p (s t) -> p s t", t=DSEQ)[:, :, 6:8]
                                    dst = tailF[:, fc, 2:2 + 2 * NSEQ].rearrange("p (s t) -> p s t", t=2)
                                    self.emit(self.ACT, lambda: nc.scalar.activation(out=dst, in_=src, func=AF.Copy), reads=[bB], writes=[tailFB])
                        for k in (1, 0):
                            for seg in t.segs:
                                self.emit(self.DVE, lambda: nc.vector.scalar_tensor_tensor(
                                    out=pview(aV, seg), in0=ve.rhs(seg, 2 - k), scalar=self.fw[l][:, f, k:k + 1], in1=pview(aV, seg),
                                    op0=ALU.mult, op1=ALU.add), reads=ve.deps(t) + [aVB, self.vecB], writes=[aVB])
                        if pending is not None:
                            pending()

                        def _fin(t=t, n=n, f=f, ge=ge, dd=dd, ddB=ddB, aV=aV, aVB=aVB):
                            bC, bCB = self.bank()
                            mms = []
                            for seg in t.segs:
                                for k in range(3):
                                    mms.append(dict(out=pview(bC, seg), lhsT=dd[:, k, :], rhs=ge.rhs(seg, 2 - k), start=(k == 0), stop=(k == 2)))
                            self.pe_group(mms, reads=ge.deps(t) + [ddB], writes=[bCB])
                            g_, g_B = gl[self.gli % 3], glB[self.gli % 3]
                            self.gli += 1
                            self.emit(self.ACT, lambda: nc.scalar.activation(out=g_[:, 0:n], in_=bC[:, 0:n], func=AF.Gelu_apprx_tanh), reads=[bCB], writes=[g_B])
                            self.emit(self.DVE, lambda: nc.vector.tensor_tensor(out=act[:, f, t.lo:t.lo + n], in0=aV[:, 0:n], in1=g_[:, 0:n], op=ALU.mult),
                                      reads=[aVB, g_B], writes=[actB[f][t.idx]])
                        pending = _fin
                    self.save_hist(G, ve, l, f, self.histf)
                    self.save_hist(G, ge, l, NF + f, self.histf)
                self.w_retire([iv, ig])
            if pending is not None:
                pending()
                pending = None

            for q in range(4):
                i0, i1, i2 = self.w_take(3, ["wd", "wd", "wd"])
                sl = [self.w_slot(i0), self.w_slot(i1), self.w_slot(i2)]
                for cc in range(2):
                    d = 2 * q + cc
                    for t in G.tiles:
                        n = t.n
                        bO, bOB = self.bank()
                        mms = []
                        for kc in range(NF):
                            slot = sl[kc // 8][0]
                            mms.append(dict(out=bO[:, 0:n], lhsT=slot[:, kc % 8, cc * 128:(cc + 1) * 128], rhs=act[:, kc, t.lo:t.lo + n],
                                            start=(kc == 0), stop=(kc == NF - 1)))
                        self.pe_group(mms, reads=[s[1] for s in sl] + [actB[kc][t.idx] for kc in range(NF)], writes=[bOB])
                        self.emit(self.DVE, lambda: nc.vector.tensor_tensor(out=x[:, d, t.lo:t.lo + n], in0=bO[:, 0:n], in1=x[:, d, t.lo:t.lo + n], op=ALU.add),
                                  reads=[bOB, xB[d][t.idx]], writes=[xB[d][t.idx]])
                self.w_retire([i0, i1, i2])

            if G.last:
                o = self.ap
                b2 = lambda c0, ncc: o["nfp"][l, :, c0 * 128:(c0 + ncc) * 128]
                self.rows_out(lambda k: (tailF[:, k, 0:2], tailFB), 2, 2 * NF, [(0, 2, b2)])
                bs = lambda c0, ncc: o["nfs"][l, :, :, c0 * 128:(c0 + ncc) * 128].rearrange("s r d -> (s r) d")
                self.rows_out(lambda k: (tailF[:, k, 2:2 + 2 * NSEQ], tailFB), 2 * NSEQ, 2 * NF, [(0, 2 * NSEQ, bs)])
            self.barrier()

    def barrier(self):
        engs = [self.PE, self.ACT, self.DVE, self.POOL, self.SP]
        toks = [(e.sem, e.sem.count) for e in engs if e.sem.count > 0]
        toks += [(s, s.count) for s in self.dsems + self.wsems if s.count > 0]
        for e in engs:
            self.wait(e, [t for t in toks if t[0] is not e.sem])

    def build(self, stage=99):
        nc = self.nc
        es = self.es
        dt_in = lambda name, shape: nc.dram_tensor(name, shape, F32, kind="ExternalInput").ap()
        dt_out = lambda name, shape: nc.dram_tensor(name, shape, F32, kind="ExternalOutput").ap()
        a = {}
        a["x_prompt"] = dt_in("x_prompt", [SEQ, D])
        a["x_sample"] = dt_in("x_sample", [NSEQ * DSEQ, D])
        a["state_conv"] = dt_in("state_conv", [DEPTH, NSEQ, 30, D])
        a["state_pool"] = dt_in("state_pool", [DEPTH, NSEQ, 15, D])
        a["state_ffn"] = dt_in("state_ffn", [DEPTH, NSEQ, 2, 2 * DFF])
        a["meta_tokens"] = dt_in("meta_tokens", [NMETA, D])
        for nm in ("norm1_g", "conv_b", "ln_g", "ln_b", "pool_scale", "norm2_g"):
            a[nm] = dt_in(nm, [DEPTH, D])
        a["final_g"] = dt_in("final_g", [1, D])
        a["w_in"] = dt_in("w_in", [DEPTH, D, 5 * D])
        a["conv_dw"] = dt_in("conv_dw", [DEPTH, CW, D])
        a["w_conv_out"] = dt_in("w_conv_out", [DEPTH, D, D])
        a["w_pool"] = dt_in("w_pool", [DEPTH, 4, 256, 256])
        a["w_out"] = dt_in("w_out", [DEPTH, D, D])
        a["w_up"] = dt_in("w_up", [DEPTH, D, 2 * DFF])
        a["ffn_dw"] = dt_in("ffn_dw", [DEPTH, 3, 2 * DFF])
        a["w_down"] = dt_in("w_down", [DEPTH, DFF, D])
        a["y_prompt"] = dt_out("y_prompt", [SEQ, D])
        a["y_sample"] = dt_out("y_sample", [NSEQ * DSEQ, D])
        a["ncp"] = dt_out("ncp", [DEPTH, 30, D])
        a["npp"] = dt_out("npp", [DEPTH, 15, D])
        a["nfp"] = dt_out("nfp", [DEPTH, 2, 2 * DFF])
        a["ncs"] = dt_out("ncs", [DEPTH, NSEQ, 30, D])
        a["nps"] = dt_out("nps", [DEPTH, NSEQ, 15, D])
        a["nfs"] = dt_out("nfs", [DEPTH, NSEQ, 2, 2 * DFF])
        self.ap = a

        self.PE = Eng("pe", nc.tensor, self.mksem("s_pe"))
        self.ACT = Eng("act", nc.scalar, self.mksem("s_act"))
        self.DVE = Eng("dve", nc.vector, self.mksem("s_dve"))
        self.POOL = Eng("pool", nc.gpsimd, self.mksem("s_pool"))
        self.SP = Eng("sp", nc.sync, self.mksem("s_sp"))
        self.dsems = [self.mksem(f"s_d{i}") for i in range(NDMASEM)]
        self.wsems = [self.mksem(f"s_w{i}") for i in range(NSLOT)]
        self.dsem_i = 0
        self.wsem_i = 0
        self.bank_i = 0
        self.st_i = self.alt = self.sq_i = 0
        self.in_group = False
        self.reserved = set()
        import collections as _c
        self.deferq = _c.deque()

        self.banks = [es.enter_context(nc.psum_tensor(f"bank{i}", [128, 512], F32)) for i in range(8)]
        self.bankB = [Buf(psum=True) for _ in range(8)]
        self.wslot = [self.sb("wslot", [128, 8, WCOL], BF16) for _ in range(NSLOT)]
        self.wslotB = [Buf() for _ in range(NSLOT)]
        self.stage = [self.sb("stage", [128, 1024], F32) for _ in range(3)]
        self.stageB = [Buf() for _ in range(3)]

        self.constB = Buf()
        self.identB = Buf()
        self.vecB = Buf()
        self.histB = Buf()
        self.ident = self.sb("ident", [128, 128], F32)
        self.identb = self.sb("identb", [128, 128], BF16)
        self.onesD = self.sb("onesD", [128, 128], BF16)
        self.epsc = self.sb("epsc", [128, 1], F32)
        self.pd = [[self.sb("pd", [128, 128], BF16) for _ in range(2)] for _ in range(4)]
        self.rr = [self.sb("rr", [128, 16], F32) for _ in range(4)]
        self.rm1 = [self.sb("rm1", [128, 16], F32) for _ in range(4)]
        self.vecs = self.sb("vecs", [128, NCH, NVEC], F32)
        self.cw = [self.sb("cw", [128, NCH, CW], F32) for _ in range(DEPTH)]
        self.fw = [self.sb("fw", [128, 2 * NF, 3], F32) for _ in range(DEPTH)]
        self.hista = [self.sb("hista", [128, NCH, 30], BF16) for _ in range(DEPTH)]
        self.histz = [self.sb("histz", [128, NCH, 15], BF16) for _ in range(DEPTH)]
        self.histf = [self.sb("histf", [128, 2 * NF, 2], BF16) for _ in range(DEPTH)]

        P = self.POOL
        self.emit(P, lambda: nc.gpsimd.memset(self.ident[:], 0.0), writes=[self.identB])
        self.emit(P, lambda: nc.gpsimd.affine_select(out=self.ident[:], in_=self.ident[:], pattern=[[-1, 128]], compare_op=ALU.not_equal,
                                                     fill=1.0, base=0, channel_multiplier=1), reads=[self.identB], writes=[self.identB])
        self.emit(P, lambda: nc.gpsimd.tensor_copy(out=self.identb[:], in_=self.ident[:]), reads=[self.identB], writes=[self.constB])
        self.emit(P, lambda: nc.gpsimd.memset(self.onesD[:], 1.0 / D), writes=[self.constB])
        self.emit(P, lambda: nc.gpsimd.memset(self.epsc[:], EPS), writes=[self.constB])
        for g, win in enumerate(PWIN):
            self.emit(P, lambda: nc.gpsimd.tensor_scalar(out=self.pd[g][0][:], in0=self.ident[:], scalar1=1.0 / win - 1.0, scalar2=None, op0=ALU.mult),
                      reads=[self.identB], writes=[self.constB])
            self.emit(P, lambda: nc.gpsimd.tensor_scalar(out=self.pd[g][1][:], in0=self.ident[:], scalar1=1.0 / win, scalar2=None, op0=ALU.mult),
                      reads=[self.identB], writes=[self.constB])
            self.emit(P, lambda: nc.gpsimd.memset(self.rr[g][:], 1.0), writes=[self.constB])
            self.emit(P, lambda: nc.gpsimd.memset(self.rm1[g][:], 0.0), writes=[self.constB])
            for t in range(win - 1):
                r = float(win) / float(t + 1)
                self.emit(P, lambda: nc.gpsimd.memset(self.rr[g][:, t:t + 1], r), writes=[self.constB])
                self.emit(P, lambda: nc.gpsimd.memset(self.rm1[g][:, t:t + 1], r - 1.0), writes=[self.constB])

        for l in range(DEPTH):
            self.dma(self.SP, out=a["ncs"][l, :, 0:22, :], in_=a["state_conv"][l, :, 8:30, :])
            self.dma(self.SP, out=a["nps"][l, :, 0:7, :], in_=a["state_pool"][l, :, 8:15, :])

        vrows = [("norm1_g", 0), ("norm1_g", 1), ("conv_b", 0), ("conv_b", 1), ("ln_g", 0), ("ln_g", 1), ("ln_b", 0), ("ln_b", 1),
                 ("pool_scale", 0), ("pool_scale", 1), ("norm2_g", 0), ("norm2_g", 1), ("final_g", 0)]
        self.rows_in([(a[nm][l:l + 1, :], i, 1) for i, (nm, l) in enumerate(vrows)], NVEC, NCH,
                     lambda k: (self.vecs[:, k, :], self.vecB))
        for l in range(DEPTH):
            self.rows_in(a["conv_dw"][l, :, :], CW, NCH, lambda k, l=l: (self.cw[l][:, k, :], self.vecB))
            self.rows_in(a["ffn_dw"][l, :, :], 3, 2 * NF, lambda k, l=l: (self.fw[l][:, k, :], self.vecB))

        groups = [Group(0, 0, 1024, False, True, False), Group(1, 1024, 1040, True, False, True)]
        self.plan_weights(len(groups))
        if stage >= 4:
            self.w_issue_upto(PREF - 1)
        if stage <= 1:
            groups = []

        self.in_group = False
        for G in groups:
            self.in_group = True
            with ExitStack() as ges:
                TG = G.tg
                x = CT(self, "x", NCH, TG, F32, ges)
                h = CT(self, "h", NCH, TG, BF16, ges)
                xB = [[Buf() for _ in G.tiles] for _ in range(NCH)]
                hB = [[Buf() for _ in G.tiles] for _ in range(NCH)]

                def xbufs(lo, n):
                    return [t.idx for t in G.tiles if t.lo < lo + n and lo < t.lo + t.n]

                col = 0
                while col < G.pw:
                    pos = G.p0 + col
                    if pos < NMETA:
                        R = NMETA - pos
                        src = a["meta_tokens"][pos:pos + R, :]
                    else:
                        R = min(128, G.pw - col)
                        src = a["x_prompt"][pos - NMETA:pos - NMETA + R, :]
                    tis = xbufs(col, R)
                    assert len(tis) >= 1
                    if len(tis) == 1:
                        self.rows_in(src, R, NCH, lambda k, col=col, R=R, ti=tis[0]: (x[:, k, col:col + R], xB[k][ti]))
                    else:
                        cut = G.tiles[tis[1]].lo - col
                        self.rows_in(src[0:cut, :], cut, NCH, lambda k, col=col, cut=cut, ti=tis[0]: (x[:, k, col:col + cut], xB[k][ti]))
                        self.rows_in(src[cut:R, :], R - cut, NCH,
                                     lambda k, col=col, cut=cut, R=R, ti=tis[1]: (x[:, k, col + cut:col + R], xB[k][ti]))
                    col += R
                if G.has_s:
                    ti = xbufs(G.pw, 128)
                    assert len(ti) == 1
                    self.rows_in(a["x_sample"][:, :], 128, NCH, lambda k, ti=ti[0]: (x[:, k, G.pw:G.pw + 128], xB[k][ti]))

                for l in range(DEPTH):
                    if stage >= 4 + 2 * l + 4 * G.idx:
                        self.mixer(G, l, x, xB, h, hB)
                    if stage >= 5 + 2 * l + 4 * G.idx:
                        self.ffn(G, l, x, xB, h, hB)

                with ExitStack() as fes:
                    f = [self.sb("ft", [128, 512], F32, fes) for _ in range(2)]
                    fB = [Buf(), Buf()]
                    sq = [self.sb("sq", [128, 512], BF16, fes) for _ in range(3)]
                    sqB = [Buf() for _ in range(3)]
                    for t in (G.tiles if stage >= 3 else []):
                        n = t.n
                        bS, bSB = self.bank()
                        for c in range(NCH):
                            s_, s_B = sq[c % 3], sqB[c % 3]
                            self.emit(self.ACT, lambda: nc.scalar.activation(out=s_[:, 0:n], in_=x[:, c, t.lo:t.lo + n], func=AF.Square),
                                      reads=[xB[c][t.idx]], writes=[s_B])
                            self.pe_group([dict(out=bS[:, 0:n], lhsT=self.onesD[:, :], rhs=s_[:, 0:n], start=(c == 0), stop=(c == NCH - 1))],
                                          reads=[s_B, self.constB], writes=[bSB])
                        self.emit(self.ACT, lambda: nc.scalar.activation(out=f[0][:, 0:n], in_=bS[:, 0:n], func=AF.Sqrt, bias=self.epsc[:, 0:1]),
                                  reads=[bSB, self.constB], writes=[fB[0]])
                        self.emit(self.DVE, lambda: nc.vector.reciprocal(out=f[1][:, 0:n], in_=f[0][:, 0:n]), reads=[fB[0]], writes=[fB[1]])
                        for c in range(NCH):
                            self.emit(self.DVE, lambda: nc.vector.scalar_tensor_tensor(
                                out=x[:, c, t.lo:t.lo + n], in0=x[:, c, t.lo:t.lo + n], scalar=self.vecs[:, c, V_FG:V_FG + 1],
                                in1=f[1][:, 0:n], op0=ALU.mult, op1=ALU.mult), reads=[xB[c][t.idx], fB[1], self.vecB], writes=[xB[c][t.idx]])
                    col = 0
                    while col < G.pw:
                        pos = G.p0 + col
                        if pos < NMETA:
                            col += NMETA - pos
                            continue
                        R = min(128, G.pw - col)
                        tis = xbufs(col, R)
                        if len(tis) > 1:
                            R = G.tiles[tis[1]].lo - col
                            tis = tis[:1]
                        r0 = pos - NMETA
                        self.rows_out(lambda k, col=col, R=R, ti=tis[0]: (x[:, k, col:col + R], xB[k][ti]), R, NCH,
                                      [(0, R, (lambda c0, ncc, r0=r0, R=R: a["y_prompt"][r0:r0 + R, c0 * 128:(c0 + ncc) * 128]))])
                        col += R
                    if G.has_s:
                        ti = xbufs(G.pw, 128)[0]
                        self.rows_out(lambda k, ti=ti: (x[:, k, G.pw:G.pw + 128], xB[k][ti]), 128, NCH,
                                      [(0, 128, (lambda c0, ncc: a["y_sample"][:, c0 * 128:(c0 + ncc) * 128]))])
                    self.barrier()
        assert stage < 99 or self.w_taken == len(self.plan), (self.w_taken, len(self.plan))
        self.barrier()
        self.es.close()
        return nc


_CACHE = {}


def kernel(x_prompt, x_sample, state_conv, state_pool, state_ffn, meta_tokens,
           norm1_g, w_in, conv_dw, conv_b, ln_g, ln_b, w_conv_out, w_pool, pool_scale,
           w_out, norm2_g, w_up, ffn_dw, w_down, final_g):
    n = 8
    f = lambda v: np.ascontiguousarray(np.asarray(v, dtype=np.float32))
    nc = K().build()
    shared = {
        "meta_tokens": f(meta_tokens), "norm1_g": f(norm1_g), "conv_b": f(conv_b), "ln_g": f(ln_g), "ln_b": f(ln_b),
        "pool_scale": f(pool_scale), "norm2_g": f(norm2_g), "final_g": f(final_g).reshape(1, D),
        "w_in": f(w_in), "conv_dw": f(conv_dw), "w_conv_out": f(w_conv_out), "w_pool": f(w_pool), "w_out": f(w_out),
        "w_up": f(w_up), "ffn_dw": f(ffn_dw), "w_down": f(w_down),
    }
    xp, xs, sc, sp, sf = f(x_prompt), f(x_sample), f(state_conv), f(state_pool), f(state_ffn)
    in_maps = []
    for b in range(n):
        m = dict(shared)
        m["x_prompt"] = xp[b]
        m["x_sample"] = np.ascontiguousarray(xs[NSEQ * b:NSEQ * (b + 1)].reshape(NSEQ * DSEQ, D))
        m["state_conv"] = np.ascontiguousarray(sc[:, NSEQ * b:NSEQ * (b + 1)])
        m["state_pool"] = np.ascontiguousarray(sp[:, NSEQ * b:NSEQ * (b + 1)])
        m["state_ffn"] = np.ascontiguousarray(sf[:, NSEQ * b:NSEQ * (b + 1)])
        in_maps.append(m)
    res = run_bass_kernel_spmd(nc, in_maps, core_ids=list(range(n)))
    r = res.results
    y_prompt = np.stack([r[b]["y_prompt"] for b in range(n)], axis=0)
    y_sample = np.concatenate([r[b]["y_sample"].reshape(NSEQ, DSEQ, D) for b in range(n)], axis=0)
    ncp = np.stack([r[b]["ncp"] for b in range(n)], axis=1)
    npp = np.stack([r[b]["npp"] for b in range(n)], axis=1)
    nfp = np.stack([r[b]["nfp"] for b in range(n)], axis=1)
    ncs = np.concatenate([r[b]["ncs"] for b in range(n)], axis=1)
    nps = np.concatenate([r[b]["nps"] for b in range(n)], axis=1)
    nfs = np.concatenate([r[b]["nfs"] for b in range(n)], axis=1)
    return (y_prompt, y_sample, ncp, npp, nfp, ncs, nps, nfs)
```

```python
import numpy as np
from contextlib import ExitStack
import concourse.bass as bass
import concourse.mybir as mybir
from concourse.bass_utils import run_bass_kernel_spmd

F32 = mybir.dt.float32
BF16 = mybir.dt.bfloat16
AF = mybir.ActivationFunctionType
ALU = mybir.AluOpType

D = 1024
NCH = 8
SEQ = 2048
NMETA = 16
NSEQ = 16
DSEQ = 8
DEPTH = 2
DFF = 2816
NF = 22
CW = 31
PWIN = (2, 4, 8, 16)
EPS = 1e-6
WCOL = 256
NSLOT = 6
PREF = 3
NDMASEM = 24

V_N1G, V_CB, V_LNG, V_LNB, V_PS, V_N2G, V_FG = 0, 2, 4, 6, 8, 10, 12
NVEC = 13


class Sem:
    def __init__(self, handle, name):
        self.handle = handle
        self.name = name
        self.count = 0


class Eng:
    def __init__(self, name, obj, sem):
        self.name = name
        self.obj = obj
        self.sem = sem
        self.waited = {}


class Buf:
    __slots__ = ("w", "r", "psum")

    def __init__(self, psum=False):
        self.w = None
        self.r = {}
        self.psum = psum


class Seg:
    def __init__(self, kind, plo, n, pc):
        self.kind, self.plo, self.n, self.pc = kind, plo, n, pc


class Tile:
    def __init__(self, idx, lo, n, segs):
        self.idx, self.lo, self.n, self.segs = idx, lo, n, segs


class Group:
    def __init__(self, idx, p0, pw, has_s, first, last):
        self.idx, self.p0, self.pw, self.has_s, self.first, self.last = idx, p0, pw, has_s, first, last
        self.tg = pw + (NSEQ * DSEQ if has_s else 0)
        self.tiles = []
        lo = 0
        ti = 0
        while lo + 512 <= pw:
            self.tiles.append(Tile(ti, lo, 512, [Seg("P", lo, 512, 0)]))
            lo += 512
            ti += 1
        segs = []
        n = 0
        if lo < pw:
            segs.append(Seg("P", lo, pw - lo, 0))
            n = pw - lo
        if has_s:
            segs.append(Seg("S", 0, NSEQ * DSEQ, n))
            n += NSEQ * DSEQ
        if segs:
            self.tiles.append(Tile(ti, lo, n, segs))


class Ext:
    def __init__(self, K, name, H, G, es):
        self.H, self.G = H, G
        self.pwid = H + G.pw
        self.swid = H + DSEQ
        tot = self.pwid + (NSEQ * self.swid if G.has_s else 0)
        self.t = K.sb(name, [128, tot], BF16, es)
        self.B = [Buf() for _ in G.tiles]
        self.Bh = Buf()

    def sview(self):
        return self.t[:, self.pwid:self.pwid + NSEQ * self.swid].rearrange("p (s w) -> p s w", w=self.swid)

    def dst(self, seg):
        if seg.kind == "P":
            return self.t[:, self.H + seg.plo:self.H + seg.plo + seg.n]
        return self.sview()[:, :, self.H:self.H + DSEQ]

    def rhs(self, seg, back):
        if seg.kind == "P":
            a = self.H + seg.plo - back
            return self.t[:, a:a + seg.n]
        return self.sview()[:, :, self.H - back:self.H - back + DSEQ]

    def deps(self, t):
        d = [self.B[t.idx], self.Bh]
        if t.idx > 0:
            d.append(self.B[t.idx - 1])
        return d


class CT:
    def __init__(self, K, name, nch, width, dt, es):
        self.ts = [K.sb(f"{name}{c}", [128, width], dt, es) for c in range(nch)]

    def __getitem__(self, key):
        p, c, sl = key
        return self.ts[c][:, sl]


def pview(bank, seg):
    if seg.kind == "P":
        return bank[:, seg.pc:seg.pc + seg.n]
    return bank[:, seg.pc:seg.pc + seg.n].rearrange("p (s t) -> p s t", t=DSEQ)


class K:
    def __init__(self):
        self.nc = bass.Bass("TRN2", target_bir_lowering=False)
        self.es = ExitStack()
        self.uid = 0

    def sb(self, name, shape, dt, es=None):
        self.uid += 1
        return (es or self.es).enter_context(self.nc.sbuf_tensor(f"{name}_{self.uid}", shape, dt))

    def mksem(self, name):
        return Sem(self.es.enter_context(self.nc.semaphore(name)), name)

    def wait(self, eng, toks):
        best = {}
        for s, v in toks:
            if best.get(s, 0) < v:
                best[s] = v
        for s, v in best.items():
            if eng.waited.get(s, 0) < v:
                eng.obj.wait_ge(s.handle, v)
                eng.waited[s] = v

    def _deps(self, eng, reads, writes):
        deps = []
        for b in reads:
            if b.w is not None:
                deps.append(b.w)
            if b.psum and eng is not self.PE:
                for s, v in b.r.items():
                    if s is not eng.sem and s is not self.PE.sem:
                        deps.append((s, v))
        skip_own = eng is self.PE
        for b in writes:
            if b.w is not None and not (skip_own and b.w[0] is eng.sem):
                deps.append(b.w)
            for s, v in b.r.items():
                if not (skip_own and s is eng.sem):
                    deps.append((s, v))
        return deps

    def emit(self, eng, fn, reads=(), writes=()):
        self.wait(eng, self._deps(eng, reads, writes))
        ins = fn()
        eng.sem.count += 1
        ins.then_inc(eng.sem.handle, 1)
        tok = (eng.sem, eng.sem.count)
        for b in writes:
            b.w = tok
            b.r = {}
        for b in reads:
            b.r[eng.sem] = tok[1]
        return tok

    def pe_group(self, mms, reads, writes):
        eng = self.PE
        self.wait(eng, self._deps(eng, reads, writes))
        ins = None
        for m in mms:
            ins = self.nc.tensor.matmul(m["out"], lhsT=m["lhsT"], rhs=m["rhs"], start=m["start"], stop=m["stop"])
        eng.sem.count += 1
        ins.then_inc(eng.sem.handle, 1)
        tok = (eng.sem, eng.sem.count)
        for b in writes:
            b.w = tok
            b.r = {}
        for b in reads:
            b.r[eng.sem] = tok[1]
        return tok

    def pe_transposes(self, trs, reads, writes):
        eng = self.PE
        self.wait(eng, self._deps(eng, reads, writes))
        ins = None
        for (o, i, idn) in trs:
            ins = self.nc.tensor.transpose(o, i, idn)
        eng.sem.count += 1
        ins.then_inc(eng.sem.handle, 1)
        tok = (eng.sem, eng.sem.count)
        for b in writes:
            b.w = tok
            b.r = {}
        for b in reads:
            b.r[eng.sem] = tok[1]
        return tok

    def dma(self, eng, out, in_, reads=(), writes=()):
        if eng is self.POOL:
            s = self.wsems[self.wsem_i % len(self.wsems)]
            self.wsem_i += 1
        else:
            s = self.dsems[self.dsem_i % len(self.dsems)]
            self.dsem_i += 1
        deps = self._deps(eng, reads, writes)
        if s.count > 0:
            deps.append((s, s.count))
        self.wait(eng, deps)
        ins = eng.obj.dma_start(out=out, in_=in_)
        s.count += 16
        ins.then_inc(s.handle, 16)
        tok = (s, s.count)
        for b in writes:
            b.w = tok
            b.r = {}
        for b in reads:
            b.r[s] = tok[1]
        return tok

    def bank(self):
        while True:
            i = self.bank_i % 8
            self.bank_i += 1
            if i not in self.reserved:
                return self.banks[i], self.bankB[i]

    def drain(self, n):
        while n > 0 and self.deferq:
            self.deferq.popleft()()
            n -= 1

    def plan_weights(self, ngroups):
        a = self.ap
        plan = []
        for g in range(ngroups):
            for l in range(DEPTH):
                win, wco, wp, wo, wup, wdn = a["w_in"], a["w_conv_out"], a["w_pool"], a["w_out"], a["w_up"], a["w_down"]
                for q in range(4):
                    plan.append((("za", g, l, q), win[l, :, 256 * q:256 * (q + 1)], 8))
                    plan.append((("zg", g, l, q), win[l, :, 1024 + 256 * q:1024 + 256 * (q + 1)], 8))
                for q in range(4):
                    plan.append((("zp", g, l, q), win[l, :, 2048 + 256 * q:2048 + 256 * (q + 1)], 8))
                    plan.append((("wp", g, l, q), wp[l, q, :, :], 2))
                    plan.append((("gb", g, l, q), win[l, :, 4096 + 256 * q:4096 + 256 * (q + 1)], 8))
                for q in range(4):
                    plan.append((("wco", g, l, q), wco[l, :, 256 * q:256 * (q + 1)], 8))
                    plan.append((("ga", g, l, q), win[l, :, 3072 + 256 * q:3072 + 256 * (q + 1)], 8))
                for q in range(4):
                    plan.append((("wout", g, l, q), wo[l, :, 256 * q:256 * (q + 1)], 8))
                for q in range(11):
                    plan.append((("upv", g, l, q), wup[l, :, 256 * q:256 * (q + 1)], 8))
                    plan.append((("upg", g, l, q), wup[l, :, DFF + 256 * q:DFF + 256 * (q + 1)], 8))
                for q in range(4):
                    for r in range(3):
                        r0 = 1024 * r
                        r1 = min(DFF, r0 + 1024)
                        plan.append((("wd", g, l, q, r), wdn[l, r0:r1, 256 * q:256 * (q + 1)], (r1 - r0) // 128))
        self.plan = plan
        self.w_issued = 0
        self.w_taken = 0
        self.w_done = [False] * len(plan)

    def w_issue_upto(self, idx):
        idx = min(idx, len(self.plan) - 1)
        while self.w_issued <= idx:
            i = self.w_issued
            tag, src, nkc = self.plan[i]
            if i >= NSLOT:
                assert self.w_done[i - NSLOT], (i, tag)
            s = i % NSLOT
            self.dma(self.POOL, out=self.wslot[s][:, 0:nkc, :], in_=src.rearrange("(c p) n -> p c n", p=128),
                     writes=[self.wslotB[s]])
            self.w_issued += 1

    def w_take(self, k, tags):
        first = self.w_taken
        res = []
        for j in range(k):
            i = first + j
            assert self.plan[i][0][0] == tags[j], (self.plan[i][0], tags)
            res.append(i)
        self.w_taken += k
        self.w_issue_upto(first + k - 1 + PREF)
        return res

    def w_slot(self, i):
        return self.wslot[i % NSLOT], self.wslotB[i % NSLOT]

    def w_retire(self, idxs):
        for i in idxs:
            self.w_done[i] = True

    def rows_in(self, src, R, nch, dst_fn):
        nc = self.nc
        srcs = src if isinstance(src, list) else [(src, 0, R)]
        for c0 in range(0, nch, 8):
            ncc = min(8, nch - c0)
            i = self.st_i % 3
            self.st_i += 1
            st, stB = self.stage[i], self.stageB[i]
            qe = self.SP
            self.wait(qe, self._deps(qe, (), [stB]))
            pBs = []
            for (sap, rlo, rn) in srcs:
                pB = Buf()
                self.dma(qe, out=st[rlo:rlo + rn, 0:ncc * 128], in_=sap[:, c0 * 128:(c0 + ncc) * 128], writes=[pB])
                pBs.append(pB)
            for q in range(0, ncc, 4):
                nk = min(4, ncc - q)
                bk, bB = self.bank()
                trs = [(bk[:, k * 128:k * 128 + R], st[:R, (q + k) * 128:(q + k + 1) * 128], self.ident[:R, :R]) for k in range(nk)]
                self.pe_transposes(trs, reads=[stB, self.identB] + pBs, writes=[bB])
                for k in range(nk):
                    dst, dB = dst_fn(c0 + q + k)
                    if (self.alt % 2) == 0:
                        self.emit(self.ACT, lambda: nc.scalar.activation(out=dst, in_=bk[:, k * 128:k * 128 + R], func=AF.Copy),
                                  reads=[bB], writes=[dB])
                    else:
                        self.emit(self.DVE, lambda: nc.vector.tensor_copy(out=dst, in_=bk[:, k * 128:k * 128 + R]),
                                  reads=[bB], writes=[dB])
                self.alt += 1

    def rows_out(self, src_fn, R, nch, dests):
        nc = self.nc
        for c0 in range(0, nch, 8):
            ncc = min(8, nch - c0)
            i = self.st_i % 3
            self.st_i += 1
            st, stB = self.stage[i], self.stageB[i]
            for q in range(0, ncc, 4):
                nk = min(4, ncc - q)
                bk, bB = self.bank()
                trs = []
                rd = [self.identB]
                for k in range(nk):
                    s, sB = src_fn(c0 + q + k)
                    trs.append((bk[:R, k * 128:(k + 1) * 128], s, self.ident[:, :]))
                    rd.append(sB)
                self.pe_transposes(trs, reads=rd, writes=[bB])
                if (self.alt % 2) == 0:
                    self.emit(self.ACT, lambda: nc.scalar.activation(out=st[:R, q * 128:(q + nk) * 128], in_=bk[:R, 0:nk * 128], func=AF.Copy),
                              reads=[bB], writes=[stB])
                else:
                    self.emit(self.DVE, lambda: nc.vector.tensor_copy(out=st[:R, q * 128:(q + nk) * 128], in_=bk[:R, 0:nk * 128]),
                              reads=[bB], writes=[stB])
                self.alt += 1
            self.wait(self.SP, [(self.ACT.sem, self.ACT.sem.count), (self.DVE.sem, self.DVE.sem.count)])
            for (rlo, rn, fn) in dests:
                self.dma(self.SP, out=fn(c0, ncc), in_=st[rlo:rlo + rn, 0:ncc * 128], reads=[stB])

    def mm8(self, bank, slot, cc, src, t, nkc=8):
        return [dict(out=bank[:, 0:t.n], lhsT=slot[:, kc, cc * 128:(cc + 1) * 128], rhs=src[:, kc, t.lo:t.lo + t.n],
                     start=(kc == 0), stop=(kc == nkc - 1)) for kc in range(nkc)]

    def tail_pieces(self, G, t, H):
        res = []
        if not G.last:
            return res
        for seg in t.segs:
            if seg.kind == "P":
                a = max(seg.plo, G.pw - H)
                b = seg.plo + seg.n
                if b > a:
                    res.append(("P", seg.pc + a - seg.plo, b - a, a - (G.pw - H)))
            else:
                res.append(("S", seg.pc, seg.n, H))
        return res

    def norm(self, G, x, xB, h, hB, vidx, tmp):
        nc = self.nc
        for t in G.tiles:
            n = t.n
            bS, bSB = self.bank()
            for c in range(NCH):
                sq, sqB = tmp["sq"][self.sq_i % 3], tmp["sqB"][self.sq_i % 3]
                self.sq_i += 1
                self.emit(self.ACT, lambda: nc.scalar.activation(out=sq[:, 0:n], in_=x[:, c, t.lo:t.lo + n], func=AF.Square),
                          reads=[xB[c][t.idx]], writes=[sqB])
                self.pe_group([dict(out=bS[:, 0:n], lhsT=self.onesD[:, :], rhs=sq[:, 0:n], start=(c == 0), stop=(c == NCH - 1))],
                              reads=[sqB, self.constB], writes=[bSB])
            sd, sdB = tmp["f"][0], tmp["fB"][0]
            rs, rsB = tmp["f"][1], tmp["fB"][1]
            self.emit(self.ACT, lambda: nc.scalar.activation(out=sd[:, 0:n], in_=bS[:, 0:n], func=AF.Sqrt, bias=self.epsc[:, 0:1]),
                      reads=[bSB, self.constB], writes=[sdB])
            self.emit(self.DVE, lambda: nc.vector.reciprocal(out=rs[:, 0:n], in_=sd[:, 0:n]), reads=[sdB], writes=[rsB])
            for c in range(NCH):
                self.emit(self.DVE, lambda: nc.vector.scalar_tensor_tensor(
                    out=h[:, c, t.lo:t.lo + n], in0=x[:, c, t.lo:t.lo + n], scalar=self.vecs[:, c, vidx:vidx + 1],
                    in1=rs[:, 0:n], op0=ALU.mult, op1=ALU.mult), reads=[xB[c][t.idx], rsB, self.vecB], writes=[hB[c][t.idx]])

    def fill_hist(self, G, ext, l, c, hist, sstate):
        nc = self.nc
        H = ext.H
        if G.first:
            self.emit(self.DVE, lambda: nc.vector.memset(ext.t[:, 0:H], 0.0), writes=[ext.Bh])
        else:
            self.emit(self.DVE, lambda: nc.vector.tensor_copy(out=ext.t[:, 0:H], in_=hist[l][:, c, :]),
                      reads=[self.histB], writes=[ext.Bh])
        if G.has_s:
            self.emit(self.DVE, lambda: nc.vector.tensor_copy(out=ext.sview()[:, :, 0:H], in_=sstate[:, c, :].rearrange("p (s r) -> p s r", r=H)),
                      reads=[self.sstB], writes=[ext.Bh])

    def save_hist(self, G, ext, l, c, hist):
        nc = self.nc
        H = ext.H
        if G.last:
            return
        rd = [ext.B[t.idx] for t in G.tiles]
        self.emit(self.DVE, lambda: nc.vector.tensor_copy(out=hist[l][:, c, :], in_=ext.t[:, G.pw:G.pw + H]),
                  reads=rd, writes=[self.histB])

    def mixer(self, G, l, x, xB, h, hB):
        nc = self.nc
        with ExitStack() as es:
            TG = G.tg
            cs = CT(self, "cs", NCH, TG, BF16, es)
            pm = CT(self, "pm", NCH, TG, BF16, es)
            csB = [[Buf() for _ in G.tiles] for _ in range(NCH)]
            pmB = [[Buf() for _ in G.tiles] for _ in range(NCH)]
            zext = [Ext(self, "zext", 15, G, es) for _ in range(2)]
            aext_es = None
            pbuf = CT(self, "pbuf", 2, TG, BF16, es)
            pbufB = [[Buf() for _ in G.tiles] for _ in range(2)]
            tmp = dict(
                sq=[self.sb("sq", [128, 512], BF16, es) for _ in range(3)], sqB=[Buf() for _ in range(3)],
                f=[self.sb("ft", [128, 512], F32, es) for _ in range(4)], fB=[Buf() for _ in range(4)],
            )
            sg = [self.sb("sg", [128, 512], F32, es) for _ in range(3)]
            sgB = [Buf() for _ in range(3)]
            self.sgi = 0
            lnA = [self.sb("lnA", [128, t_.n], F32, es) for t_ in G.tiles]
            lnB = [self.sb("lnB", [128, t_.n], F32, es) for t_ in G.tiles]
            lnAB = [Buf() for _ in G.tiles]
            lnBB = [Buf() for _ in G.tiles]
            et = [self.sb("et", [128, 16], F32, es) for _ in range(2)]
            etB = [Buf(), Buf()]
            sconv = spool = tailA = tailP = None
            tailAB, tailPB = Buf(), Buf()
            self.sstB = Buf()
            if G.has_s:
                sconv = self.sb("sconv", [128, NCH, NSEQ * 30], BF16, es)
                spool = self.sb("spool", [128, NCH, NSEQ * 15], BF16, es)
                tailA = self.sb("tailA", [128, NCH, 30 + 128], F32, es)
                tailP = self.sb("tailP", [128, NCH, 15 + 128], F32, es)
                sc = self.ap["state_conv"]
                sp = self.ap["state_pool"]
                for r0 in range(0, NSEQ * 30, 120):
                    s0 = r0 // 30
                    self.rows_in(sc[l, s0:s0 + 4, :, :].rearrange("s r d -> (s r) d"), 120, NCH,
                                 lambda k, s0=s0: (sconv[:, k, s0 * 30:(s0 + 4) * 30], self.sstB))
                for r0 in range(0, NSEQ * 15, 120):
                    s0 = r0 // 15
                    self.rows_in(sp[l, s0:s0 + 8, :, :].rearrange("s r d -> (s r) d"), 120, NCH,
                                 lambda k, s0=s0: (spool[:, k, s0 * 15:(s0 + 8) * 15], self.sstB))

            self.norm(G, x, xB, h, hB, V_N1G + l, tmp)

            ces = ExitStack()
            aext = [Ext(self, "aext", 30, G, ces) for _ in range(2)]
            d31 = self.sb("d31", [128, CW, 128], BF16, ces)
            d31B = Buf()
            for q in range(4):
                ia, ig = self.w_take(2, ["za", "zg"])
                (sa, saB), (sgt, sgtB) = self.w_slot(ia), self.w_slot(ig)
                for cc in range(2):
                    c = 2 * q + cc
                    ae = aext[c % 2]
                    self.emit(self.DVE, lambda: nc.vector.tensor_tensor(
                        out=d31[:], in0=self.identb[:].unsqueeze(1).to_broadcast([128, CW, 128]),
                        in1=self.cw[l][:, c, :].unsqueeze(2).to_broadcast([128, CW, 128]), op=ALU.mult),
                        reads=[self.constB, self.vecB], writes=[d31B])
                    self.fill_hist(G, ae, l, c, self.hista, sconv)
                    for t in G.tiles:
                        n = t.n
                        bA, bAB = self.bank()
                        self.pe_group(self.mm8(bA, sa, cc, h, t), reads=[saB] + [hB[kc][t.idx] for kc in range(NCH)], writes=[bAB])
                        bG, bGB = self.bank()
                        self.pe_group(self.mm8(bG, sgt, cc, h, t), reads=[sgtB] + [hB[kc][t.idx] for kc in range(NCH)], writes=[bGB])
                        s_, s_B = sg[self.sgi % 3], sgB[self.sgi % 3]
                        self.sgi += 1
                        self.emit(self.ACT, lambda: nc.scalar.activation(out=s_[:, 0:n], in_=bG[:, 0:n], func=AF.Sigmoid), reads=[bGB], writes=[s_B])
                        for seg in t.segs:
                            self.emit(self.DVE, lambda: nc.vector.tensor_tensor(out=ae.dst(seg), in0=pview(bA, seg), in1=pview(s_, seg), op=ALU.mult),
                                      reads=[bAB, s_B], writes=[ae.B[t.idx]])
                        for (kind, pl, m, tl) in self.tail_pieces(G, t, 30):
                            self.emit(self.DVE, lambda: nc.vector.tensor_tensor(out=tailA[:, c, tl:tl + m], in0=bA[:, pl:pl + m], in1=s_[:, pl:pl + m], op=ALU.mult),
                                      reads=[bAB, s_B], writes=[tailAB])
                    self.save_hist(G, ae, l, c, self.hista)
                    for t in G.tiles:
                        bC, bCB = self.bank()
                        mms = []
                        for seg in t.segs:
                            for k in range(CW):
                                mms.append(dict(out=pview(bC, seg), lhsT=d31[:, k, :], rhs=ae.rhs(seg, CW - 1 - k),
                                                start=(k == 0), stop=(k == CW - 1)))
                        self.pe_group(mms, reads=ae.deps(t) + [d31B], writes=[bCB])
                        self.emit(self.ACT, lambda: nc.scalar.activation(out=cs[:, c, t.lo:t.lo + t.n], in_=bC[:, 0:t.n], func=AF.Identity,
                                                                         bias=self.vecs[:, c, V_CB + l:V_CB + l + 1]),
                                  reads=[bCB, self.vecB], writes=[csB[c][t.idx]])
                self.w_retire([ia, ig])

            ces.close()
            lsg = [self.sb("lsg", [128, 512], F32, es) for _ in range(3)]
            lsgB = [Buf() for _ in range(3)]
            lz = [self.sb("lz", [128, 512], F32, es) for _ in range(3)]
            lzB = [Buf() for _ in range(3)]
            self.wait(self.ACT, [(self.PE.sem, self.PE.sem.count), (self.DVE.sem, self.DVE.sem.count)])
            self.reserved = {6, 7}
            bM, bMB = self.banks[6], self.bankB[6]
            bQ, bQB = self.banks[7], self.bankB[7]
            f, fB = tmp["f"], tmp["fB"]

            def s1(t, c, k):
                n = t.n
                sq, sqB = tmp["sq"][k % 3], tmp["sqB"][k % 3]
                self.emit(self.ACT, lambda: nc.scalar.activation(out=sq[:, 0:n], in_=cs[:, c, t.lo:t.lo + n], func=AF.Square),
                          reads=[csB[c][t.idx]], writes=[sqB])

            def s2(t, c, k):
                n = t.n
                sq, sqB = tmp["sq"][k % 3], tmp["sqB"][k % 3]
                self.pe_group([dict(out=bM[:, 0:n], lhsT=self.onesD[:, :], rhs=cs[:, c, t.lo:t.lo + n], start=(c == 0), stop=(c == NCH - 1))],
                              reads=[csB[c][t.idx], self.constB], writes=[bMB])
                self.pe_group([dict(out=bQ[:, 0:n], lhsT=self.onesD[:, :], rhs=sq[:, 0:n], start=(c == 0), stop=(c == NCH - 1))],
                              reads=[sqB, self.constB], writes=[bQB])

            def ab_parts(t):
                n = t.n
                A, AB_ = lnA[t.idx], lnAB[t.idx]
                Bm, BmB = lnB[t.idx], lnBB[t.idx]

                def p1():
                    self.emit(self.ACT, lambda: nc.scalar.activation(out=A[:, 0:n], in_=bM[:, 0:n], func=AF.Square), reads=[bMB], writes=[AB_])
                    self.emit(self.ACT, lambda: nc.scalar.activation(out=Bm[:, 0:n], in_=bM[:, 0:n], func=AF.Copy), reads=[bMB], writes=[BmB])
                    self.emit(self.ACT, lambda: nc.scalar.activation(out=f[0][:, 0:n], in_=bQ[:, 0:n], func=AF.Copy), reads=[bQB], writes=[fB[0]])

                def p2():
                    self.emit(self.DVE, lambda: nc.vector.tensor_tensor(out=f[0][:, 0:n], in0=f[0][:, 0:n], in1=A[:, 0:n], op=ALU.subtract),
                              reads=[fB[0], AB_], writes=[fB[0]])

                def p3():
                    self.emit(self.ACT, lambda: nc.scalar.activation(out=f[0][:, 0:n], in_=f[0][:, 0:n], func=AF.Sqrt, bias=self.epsc[:, 0:1]),
                              reads=[fB[0], self.constB], writes=[fB[0]])

                def p4():
                    self.emit(self.DVE, lambda: nc.vector.reciprocal(out=A[:, 0:n], in_=f[0][:, 0:n]), reads=[fB[0]], writes=[AB_])
                    self.emit(self.DVE, lambda: nc.vector.scalar_tensor_tensor(out=Bm[:, 0:n], in0=Bm[:, 0:n], scalar=-1.0, in1=A[:, 0:n],
                                                                               op0=ALU.mult, op1=ALU.mult), reads=[BmB, AB_], writes=[BmB])
                return p1, [p2, p3, p4]

            def a1(t, c, k):
                n = t.n
                A, AB_ = lnA[t.idx], lnAB[t.idx]
                Bm, BmB = lnB[t.idx], lnBB[t.idx]
                u, uB = f[1 + (k % 3)], fB[1 + (k % 3)]
                s_, s_B = lsg[k % 3], lsgB[k % 3]
                z_, z_B = lz[k % 3], lzB[k % 3]
                gcol = self.vecs[:, c, V_LNG + l:V_LNG + l + 1]
                bcol = self.vecs[:, c, V_LNB + l:V_LNB + l + 1]
                self.emit(self.DVE, lambda: nc.vector.tensor_tensor(out=u[:, 0:n], in0=cs[:, c, t.lo:t.lo + n], in1=A[:, 0:n], op=ALU.mult),
                          reads=[csB[c][t.idx], AB_], writes=[uB])
                self.emit(self.DVE, lambda: nc.vector.tensor_tensor(out=u[:, 0:n], in0=u[:, 0:n], in1=Bm[:, 0:n], op=ALU.add),
                          reads=[uB, BmB], writes=[uB])
                self.emit(self.ACT, lambda: nc.scalar.activation(out=s_[:, 0:n], in_=u[:, 0:n], func=AF.Sigmoid, scale=gcol, bias=bcol),
                          reads=[uB, self.vecB], writes=[s_B])
                self.emit(self.ACT, lambda: nc.scalar.activation(out=z_[:, 0:n], in_=u[:, 0:n], func=AF.Identity, scale=gcol, bias=bcol),
                          reads=[uB, self.vecB], writes=[z_B])

            def a2(t, c, k):
                n = t.n
                s_, s_B = lsg[k % 3], lsgB[k % 3]
                z_, z_B = lz[k % 3], lzB[k % 3]
                self.emit(self.DVE, lambda: nc.vector.tensor_tensor(out=cs[:, c, t.lo:t.lo + n], in0=z_[:, 0:n], in1=s_[:, 0:n], op=ALU.mult),
                          reads=[s_B, z_B], writes=[csB[c][t.idx]])

            def lagged(firsts, seconds, lag):
                out = []
                nU = len(firsts)
                for i in range(nU + lag):
                    if i < nU:
                        out.append(firsts[i])
                    if i - lag >= 0:
                        out.append(seconds[i - lag])
                return out

            def weave(units, parts, step=3):
                out = []
                parts = list(parts)
                for i, u_ in enumerate(units):
                    out.append(u_)
                    if parts and (i % step == step - 1):
                        out.append(parts.pop(0))
                return out + parts

            kk = 0
            carry = []
            for t in G.tiles:
                F1, F2 = [], []
                for c in range(NCH):
                    F1.append(lambda t=t, c=c, k=kk: s1(t, c, k))
                    F2.append(lambda t=t, c=c, k=kk: s2(t, c, k))
                    kk += 1
                self.deferq.extend(weave(lagged(F1, F2, 2), carry))
                p1, carry = ab_parts(t)
                self.deferq.append(p1)
            F1, F2 = [], []
            kk = 0
            for t in G.tiles:
                for c in range(NCH):
                    F1.append(lambda t=t, c=c, k=kk: a1(t, c, k))
                    F2.append(lambda t=t, c=c, k=kk: a2(t, c, k))
                    kk += 1
            ap_units = lagged(F1, F2, 2)
            if len(G.tiles) == 1:
                self.deferq.extend(carry + ap_units)
            else:
                self.deferq.extend(weave(ap_units, carry, step=2))

            for g in range(4):
                win = PWIN[g]
                iz, iw, ig = self.w_take(3, ["zp", "wp", "gb"])
                (sz, szB), (sw, swB), (sgb, sgbB) = self.w_slot(iz), self.w_slot(iw), self.w_slot(ig)
                for cc in range(2):
                    c = 2 * g + cc
                    ze = zext[cc]
                    self.fill_hist(G, ze, l, c, self.histz, spool)
                    for t in G.tiles:
                        bk, bB = self.bank()
                        self.pe_group(self.mm8(bk, sz, cc, h, t), reads=[szB] + [hB[kc][t.idx] for kc in range(NCH)], writes=[bB])
                        self.drain(2)
                        for seg in t.segs:
                            self.emit(self.ACT, lambda: nc.scalar.activation(out=ze.dst(seg), in_=pview(bk, seg), func=AF.Copy),
                                      reads=[bB], writes=[ze.B[t.idx]])
                        for (kind, pl, m, tl) in self.tail_pieces(G, t, 15):
                            self.emit(self.DVE, lambda: nc.vector.tensor_copy(out=tailP[:, c, tl:tl + m], in_=bk[:, pl:pl + m]),
                                      reads=[bB], writes=[tailPB])
                    self.save_hist(G, ze, l, c, self.histz)
                for cc in range(2):
                    ze = zext[cc]
                    for t in G.tiles:
                        bk, bB = self.bank()
                        mms = []
                        for seg in t.segs:
                            for j in range(win):
                                mms.append(dict(out=pview(bk, seg), lhsT=self.pd[g][0 if j == 0 else 1][:, :], rhs=ze.rhs(seg, j),
                                                start=(j == 0), stop=(j == win - 1)))
                        self.pe_group(mms, reads=ze.deps(t) + [self.constB], writes=[bB])
                        self.drain(2)
                        self.emit(self.DVE, lambda: nc.vector.tensor_copy(out=pbuf[:, cc, t.lo:t.lo + t.n], in_=bk[:, 0:t.n]),
                                  reads=[bB], writes=[pbufB[cc][t.idx]])
                        if G.first and t.idx == 0:
                            self.emit(self.DVE, lambda: nc.vector.tensor_tensor(out=et[0][:, 0:15], in0=ze.t[:, 15:30], in1=self.rm1[g][:, 0:15], op=ALU.mult),
                                      reads=ze.deps(t) + [self.constB], writes=[etB[0]])
                            self.emit(self.DVE, lambda: nc.vector.tensor_tensor(out=et[1][:, 0:15], in0=bk[:, 0:15], in1=self.rr[g][:, 0:15], op=ALU.mult),
                                      reads=[bB, self.constB], writes=[etB[1]])
                            self.emit(self.DVE, lambda: nc.vector.tensor_tensor(out=pbuf[:, cc, 0:15], in0=et[0][:, 0:15], in1=et[1][:, 0:15], op=ALU.add),
                                      reads=[etB[0], etB[1]], writes=[pbufB[cc][t.idx]])
                for oc in range(2):
                    d = 2 * g + oc
                    for t in G.tiles:
                        n = t.n
                        bP, bPB = self.bank()
                        self.pe_group(self.mm8(bP, sw, oc, pbuf, t, nkc=2), reads=[swB, pbufB[0][t.idx], pbufB[1][t.idx]], writes=[bPB])
                        bG, bGB = self.bank()
                        self.pe_group(self.mm8(bG, sgb, oc, h, t), reads=[sgbB] + [hB[kc][t.idx] for kc in range(NCH)], writes=[bGB])
                        self.drain(2)
                        s_, s_B = sg[self.sgi % 3], sgB[self.sgi % 3]
                        self.sgi += 1
                        self.emit(self.ACT, lambda: nc.scalar.activation(out=s_[:, 0:n], in_=bG[:, 0:n], func=AF.Sigmoid), reads=[bGB], writes=[s_B])
                        self.emit(self.DVE, lambda: nc.vector.scalar_tensor_tensor(
                            out=pm[:, d, t.lo:t.lo + n], in0=bP[:, 0:n], scalar=self.vecs[:, d, V_PS + l:V_PS + l + 1], in1=s_[:, 0:n],
                            op0=ALU.mult, op1=ALU.mult), reads=[bPB, s_B, self.vecB], writes=[pmB[d][t.idx]])
                self.w_retire([iz, iw, ig])

            self.drain(10 ** 6)
            self.reserved = set()

            for q in range(4):
                io, ig = self.w_take(2, ["wco", "ga"])
                (so, soB), (sga, sgaB) = self.w_slot(io), self.w_slot(ig)
                for cc in range(2):
                    d = 2 * q + cc
                    for t in G.tiles:
                        n = t.n
                        bO, bOB = self.bank()
                        self.pe_group(self.mm8(bO, so, cc, cs, t), reads=[soB] + [csB[kc][t.idx] for kc in range(NCH)], writes=[bOB])
                        bG, bGB = self.bank()
                        self.pe_group(self.mm8(bG, sga, cc, h, t), reads=[sgaB] + [hB[kc][t.idx] for kc in range(NCH)], writes=[bGB])
                        s_, s_B = sg[self.sgi % 3], sgB[self.sgi % 3]
                        self.sgi += 1
                        self.emit(self.ACT, lambda: nc.scalar.activation(out=s_[:, 0:n], in_=bG[:, 0:n], func=AF.Sigmoid), reads=[bGB], writes=[s_B])
                        self.emit(self.DVE, lambda: nc.vector.tensor_tensor(out=s_[:, 0:n], in0=bO[:, 0:n], in1=s_[:, 0:n], op=ALU.mult),
                                  reads=[bOB, s_B], writes=[s_B])
                        self.emit(self.DVE, lambda: nc.vector.tensor_tensor(out=pm[:, d, t.lo:t.lo + n], in0=s_[:, 0:n], in1=pm[:, d, t.lo:t.lo + n], op=ALU.add),
                                  reads=[s_B, pmB[d][t.idx]], writes=[pmB[d][t.idx]])
                self.w_retire([io, ig])

            for q in range(4):
                (io,) = self.w_take(1, ["wout"])
                so, soB = self.w_slot(io)
                for cc in range(2):
                    d = 2 * q + cc
                    for t in G.tiles:
                        n = t.n
                        bO, bOB = self.bank()
                        self.pe_group(self.mm8(bO, so, cc, pm, t), reads=[soB] + [pmB[kc][t.idx] for kc in range(NCH)], writes=[bOB])
                        self.emit(self.DVE, lambda: nc.vector.tensor_tensor(out=x[:, d, t.lo:t.lo + n], in0=bO[:, 0:n], in1=x[:, d, t.lo:t.lo + n], op=ALU.add),
                                  reads=[bOB, xB[d][t.idx]], writes=[xB[d][t.idx]])
                self.w_retire([io])

            if G.last:
                o = self.ap
                b30 = lambda c0, ncc: o["ncp"][l, :, c0 * 128:(c0 + ncc) * 128]
                self.rows_out(lambda k: (tailA[:, k, 0:30], tailAB), 30, NCH, [(0, 30, b30)])
                dests = [(s * DSEQ, DSEQ, (lambda c0, ncc, s=s: o["ncs"][l, s, 22:30, c0 * 128:(c0 + ncc) * 128])) for s in range(NSEQ)]
                self.rows_out(lambda k: (tailA[:, k, 30:158], tailAB), 128, NCH, dests)
                b15 = lambda c0, ncc: o["npp"][l, :, c0 * 128:(c0 + ncc) * 128]
                self.rows_out(lambda k: (tailP[:, k, 0:15], tailPB), 15, NCH, [(0, 15, b15)])
                dests = [(s * DSEQ, DSEQ, (lambda c0, ncc, s=s: o["nps"][l, s, 7:15, c0 * 128:(c0 + ncc) * 128])) for s in range(NSEQ)]
                self.rows_out(lambda k: (tailP[:, k, 15:143], tailPB), 128, NCH, dests)
            self.barrier()

    def ffn(self, G, l, x, xB, h, hB):
        nc = self.nc
        with ExitStack() as es:
            TG = G.tg
            act = CT(self, "act", NF, TG, BF16, es)
            actB = [[Buf() for _ in G.tiles] for _ in range(NF)]
            vext = [Ext(self, "vext", 2, G, es) for _ in range(2)]
            gext = [Ext(self, "gext", 2, G, es) for _ in range(2)]
            acc = [self.sb("acc", [128, 512], F32, es) for _ in range(4)]
            accB = [Buf() for _ in range(4)]
            acci = 0
            d3 = [self.sb("d3", [128, 3, 128], BF16, es) for _ in range(2)]
            d3B = [Buf(), Buf()]
            self.gli = 0
            tmp = dict(
                sq=[self.sb("sq", [128, 512], BF16, es) for _ in range(3)], sqB=[Buf() for _ in range(3)],
                f=[self.sb("ft", [128, 512], F32, es) for _ in range(2)], fB=[Buf() for _ in range(2)],
            )
            gl = [self.sb("gl", [128, 512], F32, es) for _ in range(3)]
            glB = [Buf() for _ in range(3)]
            gli = 0
            sffn = tailF = None
            tailFB = Buf()
            self.sstB = Buf()
            if G.has_s:
                sffn = self.sb("sffn", [128, 2 * NF, NSEQ * 2], BF16, es)
                tailF = self.sb("tailF", [128, 2 * NF, 2 + 2 * NSEQ], F32, es)
                sf = self.ap["state_ffn"]
                self.rows_in(sf[l, :, :, :].rearrange("s r d -> (s r) d"), 2 * NSEQ, 2 * NF,
                             lambda k: (sffn[:, k, :], self.sstB))

            self.norm(G, x, xB, h, hB, V_N2G + l, tmp)

            pending = None
            for q in range(11):
                iv, ig = self.w_take(2, ["upv", "upg"])
                (sv, svB), (sgg, sggB) = self.w_slot(iv), self.w_slot(ig)
                for cc in range(2):
                    f = 2 * q + cc
                    ve, ge = vext[f % 2], gext[f % 2]
                    dd, ddB = d3[f % 2], d3B[f % 2]
                    self.emit(self.DVE, lambda: nc.vector.tensor_tensor(
                        out=dd[:], in0=self.identb[:].unsqueeze(1).to_broadcast([128, 3, 128]),
                        in1=self.fw[l][:, NF + f, :].unsqueeze(2).to_broadcast([128, 3, 128]), op=ALU.mult),
                        reads=[self.constB, self.vecB], writes=[ddB])
                    self.fill_hist(G, ve, l, f, self.histf, sffn)
                    self.fill_hist(G, ge, l, NF + f, self.histf, sffn)
                    for t in G.tiles:
                        n = t.n
                        bV, bVB = self.bank()
                        self.pe_group(self.mm8(bV, sv, cc, h, t), reads=[svB] + [hB[kc][t.idx] for kc in range(NCH)], writes=[bVB])
                        bG, bGB = self.bank()
                        self.pe_group(self.mm8(bG, sgg, cc, h, t), reads=[sggB] + [hB[kc][t.idx] for kc in range(NCH)], writes=[bGB])
                        aV, aVB = acc[acci % 4], accB[acci % 4]
                        acci += 1
                        for (ext, bk, bB, fc) in ((ve, bV, bVB, f), (ge, bG, bGB, NF + f)):
                            for seg in t.segs:
                                self.emit(self.ACT, lambda: nc.scalar.activation(out=ext.dst(seg), in_=pview(bk, seg), func=AF.Copy),
                                          reads=[bB], writes=[ext.B[t.idx]])
                            if ext is ve:
                                self.emit(self.ACT, lambda: nc.scalar.activation(out=aV[:, 0:n], in_=bk[:, 0:n], func=AF.Identity,
                                                                                 scale=self.fw[l][:, fc, 2:3]),
                                          reads=[bB, self.vecB], writes=[aVB])
                            for (kind, pl, m, tl) in self.tail_pieces(G, t, 2):
                                if kind == "P":
                                    self.emit(self.ACT, lambda: nc.scalar.activation(out=tailF[:, fc, tl:tl + m], in_=bk[:, pl:pl + m], func=AF.Copy),
                                              reads=[bB], writes=[tailFB])
                                else:
                                    src = bk[:, pl:pl + m].rearrange("p (s t) -> p s t", t=DSEQ)[:, :, 6:8]
                                    dst = tailF[:, fc, 2:2 + 2 * NSEQ].rearrange("p (s t) -> p s t", t=2)
                                    self.emit(self.ACT, lambda: nc.scalar.activation(out=dst, in_=src, func=AF.Copy), reads=[bB], writes=[tailFB])
                        for k in (1, 0):
                            for seg in t.segs:
                                self.emit(self.DVE, lambda: nc.vector.scalar_tensor_tensor(
                                    out=pview(aV, seg), in0=ve.rhs(seg, 2 - k), scalar=self.fw[l][:, f, k:k + 1], in1=pview(aV, seg),
                                    op0=ALU.mult, op1=ALU.add), reads=ve.deps(t) + [aVB, self.vecB], writes=[aVB])
                        if pending is not None:
                            pending()

                        def _fin(t=t, n=n, f=f, ge=ge, dd=dd, ddB=ddB, aV=aV, aVB=aVB):
                            bC, bCB = self.bank()
                            mms = []
                            for seg in t.segs:
                                for k in range(3):
                                    mms.append(dict(out=pview(bC, seg), lhsT=dd[:, k, :], rhs=ge.rhs(seg, 2 - k), start=(k == 0), stop=(k == 2)))
                            self.pe_group(mms, reads=ge.deps(t) + [ddB], writes=[bCB])
                            g_, g_B = gl[self.gli % 3], glB[self.gli % 3]
                            self.gli += 1
                            self.emit(self.ACT, lambda: nc.scalar.activation(out=g_[:, 0:n], in_=bC[:, 0:n], func=AF.Gelu_apprx_tanh), reads=[bCB], writes=[g_B])
                            self.emit(self.DVE, lambda: nc.vector.tensor_tensor(out=act[:, f, t.lo:t.lo + n], in0=aV[:, 0:n], in1=g_[:, 0:n], op=ALU.mult),
                                      reads=[aVB, g_B], writes=[actB[f][t.idx]])
                        pending = _fin
                    self.save_hist(G, ve, l, f, self.histf)
                    self.save_hist(G, ge, l, NF + f, self.histf)
                self.w_retire([iv, ig])
            if pending is not None:
                pending()
                pending = None

            for q in range(4):
                i0, i1, i2 = self.w_take(3, ["wd", "wd", "wd"])
                sl = [self.w_slot(i0), self.w_slot(i1), self.w_slot(i2)]
                for cc in range(2):
                    d = 2 * q + cc
                    for t in G.tiles:
                        n = t.n
                        bO, bOB = self.bank()
                        mms = []
                        for kc in range(NF):
                            slot = sl[kc // 8][0]
                            mms.append(dict(out=bO[:, 0:n], lhsT=slot[:, kc % 8, cc * 128:(cc + 1) * 128], rhs=act[:, kc, t.lo:t.lo + n],
                                            start=(kc == 0), stop=(kc == NF - 1)))
                        self.pe_group(mms, reads=[s[1] for s in sl] + [actB[kc][t.idx] for kc in range(NF)], writes=[bOB])
                        self.emit(self.DVE, lambda: nc.vector.tensor_tensor(out=x[:, d, t.lo:t.lo + n], in0=bO[:, 0:n], in1=x[:, d, t.lo:t.lo + n], op=ALU.add),
                                  reads=[bOB, xB[d][t.idx]], writes=[xB[d][t.idx]])
                self.w_retire([i0, i1, i2])

            if G.last:
                o = self.ap
                b2 = lambda c0, ncc: o["nfp"][l, :, c0 * 128:(c0 + ncc) * 128]
                self.rows_out(lambda k: (tailF[:, k, 0:2], tailFB), 2, 2 * NF, [(0, 2, b2)])
                bs = lambda c0, ncc: o["nfs"][l, :, :, c0 * 128:(c0 + ncc) * 128].rearrange("s r d -> (s r) d")
                self.rows_out(lambda k: (tailF[:, k, 2:2 + 2 * NSEQ], tailFB), 2 * NSEQ, 2 * NF, [(0, 2 * NSEQ, bs)])
            self.barrier()

    def barrier(self):
        engs = [self.PE, self.ACT, self.DVE, self.POOL, self.SP]
        toks = [(e.sem, e.sem.count) for e in engs if e.sem.count > 0]
        toks += [(s, s.count) for s in self.dsems + self.wsems if s.count > 0]
        for e in engs:
            self.wait(e, [t for t in toks if t[0] is not e.sem])

    def build(self, stage=99):
        nc = self.nc
        es = self.es
        dt_in = lambda name, shape: nc.dram_tensor(name, shape, F32, kind="ExternalInput").ap()
        dt_out = lambda name, shape: nc.dram_tensor(name, shape, F32, kind="ExternalOutput").ap()
        a = {}
        a["x_prompt"] = dt_in("x_prompt", [SEQ, D])
        a["x_sample"] = dt_in("x_sample", [NSEQ * DSEQ, D])
        a["state_conv"] = dt_in("state_conv", [DEPTH, NSEQ, 30, D])
        a["state_pool"] = dt_in("state_pool", [DEPTH, NSEQ, 15, D])
        a["state_ffn"] = dt_in("state_ffn", [DEPTH, NSEQ, 2, 2 * DFF])
        a["meta_tokens"] = dt_in("meta_tokens", [NMETA, D])
        for nm in ("norm1_g", "conv_b", "ln_g", "ln_b", "pool_scale", "norm2_g"):
            a[nm] = dt_in(nm, [DEPTH, D])
        a["final_g"] = dt_in("final_g", [1, D])
        a["w_in"] = dt_in("w_in", [DEPTH, D, 5 * D])
        a["conv_dw"] = dt_in("conv_dw", [DEPTH, CW, D])
        a["w_conv_out"] = dt_in("w_conv_out", [DEPTH, D, D])
        a["w_pool"] = dt_in("w_pool", [DEPTH, 4, 256, 256])
        a["w_out"] = dt_in("w_out", [DEPTH, D, D])
        a["w_up"] = dt_in("w_up", [DEPTH, D, 2 * DFF])
        a["ffn_dw"] = dt_in("ffn_dw", [DEPTH, 3, 2 * DFF])
        a["w_down"] = dt_in("w_down", [DEPTH, DFF, D])
        a["y_prompt"] = dt_out("y_prompt", [SEQ, D])
        a["y_sample"] = dt_out("y_sample", [NSEQ * DSEQ, D])
        a["ncp"] = dt_out("ncp", [DEPTH, 30, D])
        a["npp"] = dt_out("npp", [DEPTH, 15, D])
        a["nfp"] = dt_out("nfp", [DEPTH, 2, 2 * DFF])
        a["ncs"] = dt_out("ncs", [DEPTH, NSEQ, 30, D])
        a["nps"] = dt_out("nps", [DEPTH, NSEQ, 15, D])
        a["nfs"] = dt_out("nfs", [DEPTH, NSEQ, 2, 2 * DFF])
        self.ap = a

        self.PE = Eng("pe", nc.tensor, self.mksem("s_pe"))
        self.ACT = Eng("act", nc.scalar, self.mksem("s_act"))
        self.DVE = Eng("dve", nc.vector, self.mksem("s_dve"))
        self.POOL = Eng("pool", nc.gpsimd, self.mksem("s_pool"))
        self.SP = Eng("sp", nc.sync, self.mksem("s_sp"))
        self.dsems = [self.mksem(f"s_d{i}") for i in range(NDMASEM)]
        self.wsems = [self.mksem(f"s_w{i}") for i in range(NSLOT)]
        self.dsem_i = 0
        self.wsem_i = 0
        self.bank_i = 0
        self.st_i = self.alt = self.sq_i = 0
        self.in_group = False
        self.reserved = set()
        import collections as _c
        self.deferq = _c.deque()

        self.banks = [es.enter_context(nc.psum_tensor(f"bank{i}", [128, 512], F32)) for i in range(8)]
        self.bankB = [Buf(psum=True) for _ in range(8)]
        self.wslot = [self.sb("wslot", [128, 8, WCOL], BF16) for _ in range(NSLOT)]
        self.wslotB = [Buf() for _ in range(NSLOT)]
        self.stage = [self.sb("stage", [128, 1024], F32) for _ in range(3)]
        self.stageB = [Buf() for _ in range(3)]

        self.constB = Buf()
        self.identB = Buf()
        self.vecB = Buf()
        self.histB = Buf()
        self.ident = self.sb("ident", [128, 128], F32)
        self.identb = self.sb("identb", [128, 128], BF16)
        self.onesD = self.sb("onesD", [128, 128], BF16)
        self.epsc = self.sb("epsc", [128, 1], F32)
        self.pd = [[self.sb("pd", [128, 128], BF16) for _ in range(2)] for _ in range(4)]
        self.rr = [self.sb("rr", [128, 16], F32) for _ in range(4)]
        self.rm1 = [self.sb("rm1", [128, 16], F32) for _ in range(4)]
        self.vecs = self.sb("vecs", [128, NCH, NVEC], F32)
        self.cw = [self.sb("cw", [128, NCH, CW], F32) for _ in range(DEPTH)]
        self.fw = [self.sb("fw", [128, 2 * NF, 3], F32) for _ in range(DEPTH)]
        self.hista = [self.sb("hista", [128, NCH, 30], BF16) for _ in range(DEPTH)]
        self.histz = [self.sb("histz", [128, NCH, 15], BF16) for _ in range(DEPTH)]
        self.histf = [self.sb("histf", [128, 2 * NF, 2], BF16) for _ in range(DEPTH)]

        P = self.POOL
        self.emit(P, lambda: nc.gpsimd.memset(self.ident[:], 0.0), writes=[self.identB])
        self.emit(P, lambda: nc.gpsimd.affine_select(out=self.ident[:], in_=self.ident[:], pattern=[[-1, 128]], compare_op=ALU.not_equal,
                                                     fill=1.0, base=0, channel_multiplier=1), reads=[self.identB], writes=[self.identB])
        self.emit(P, lambda: nc.gpsimd.tensor_copy(out=self.identb[:], in_=self.ident[:]), reads=[self.identB], writes=[self.constB])
        self.emit(P, lambda: nc.gpsimd.memset(self.onesD[:], 1.0 / D), writes=[self.constB])
        self.emit(P, lambda: nc.gpsimd.memset(self.epsc[:], EPS), writes=[self.constB])
        for g, win in enumerate(PWIN):
            self.emit(P, lambda: nc.gpsimd.tensor_scalar(out=self.pd[g][0][:], in0=self.ident[:], scalar1=1.0 / win - 1.0, scalar2=None, op0=ALU.mult),
                      reads=[self.identB], writes=[self.constB])
            self.emit(P, lambda: nc.gpsimd.tensor_scalar(out=self.pd[g][1][:], in0=self.ident[:], scalar1=1.0 / win, scalar2=None, op0=ALU.mult),
                      reads=[self.identB], writes=[self.constB])
            self.emit(P, lambda: nc.gpsimd.memset(self.rr[g][:], 1.0), writes=[self.constB])
            self.emit(P, lambda: nc.gpsimd.memset(self.rm1[g][:], 0.0), writes=[self.constB])
            for t in range(win - 1):
                r = float(win) / float(t + 1)
                self.emit(P, lambda: nc.gpsimd.memset(self.rr[g][:, t:t + 1], r), writes=[self.constB])
                self.emit(P, lambda: nc.gpsimd.memset(self.rm1[g][:, t:t + 1], r - 1.0), writes=[self.constB])

        for l in range(DEPTH):
            self.dma(self.SP, out=a["ncs"][l, :, 0:22, :], in_=a["state_conv"][l, :, 8:30, :])
            self.dma(self.SP, out=a["nps"][l, :, 0:7, :], in_=a["state_pool"][l, :, 8:15, :])

        vrows = [("norm1_g", 0), ("norm1_g", 1), ("conv_b", 0), ("conv_b", 1), ("ln_g", 0), ("ln_g", 1), ("ln_b", 0), ("ln_b", 1),
                 ("pool_scale", 0), ("pool_scale", 1), ("norm2_g", 0), ("norm2_g", 1), ("final_g", 0)]
        self.rows_in([(a[nm][l:l + 1, :], i, 1) for i, (nm, l) in enumerate(vrows)], NVEC, NCH,
                     lambda k: (self.vecs[:, k, :], self.vecB))
        for l in range(DEPTH):
            self.rows_in(a["conv_dw"][l, :, :], CW, NCH, lambda k, l=l: (self.cw[l][:, k, :], self.vecB))
            self.rows_in(a["ffn_dw"][l, :, :], 3, 2 * NF, lambda k, l=l: (self.fw[l][:, k, :], self.vecB))

        groups = [Group(0, 0, 1024, False, True, False), Group(1, 1024, 1040, True, False, True)]
        self.plan_weights(len(groups))
        if stage >= 4:
            self.w_issue_upto(PREF - 1)
        if stage <= 1:
            groups = []

        self.in_group = False
        for G in groups:
            self.in_group = True
            with ExitStack() as ges:
                TG = G.tg
                x = CT(self, "x", NCH, TG, F32, ges)
                h = CT(self, "h", NCH, TG, BF16, ges)
                xB = [[Buf() for _ in G.tiles] for _ in range(NCH)]
                hB = [[Buf() for _ in G.tiles] for _ in range(NCH)]

                def xbufs(lo, n):
                    return [t.idx for t in G.tiles if t.lo < lo + n and lo < t.lo + t.n]

                col = 0
                while col < G.pw:
                    pos = G.p0 + col
                    if pos < NMETA:
                        R = NMETA - pos
                        src = a["meta_tokens"][pos:pos + R, :]
                    else:
                        R = min(128, G.pw - col)
                        src = a["x_prompt"][pos - NMETA:pos - NMETA + R, :]
                    tis = xbufs(col, R)
                    assert len(tis) >= 1
                    if len(tis) == 1:
                        self.rows_in(src, R, NCH, lambda k, col=col, R=R, ti=tis[0]: (x[:, k, col:col + R], xB[k][ti]))
                    else:
                        cut = G.tiles[tis[1]].lo - col
                        self.rows_in(src[0:cut, :], cut, NCH, lambda k, col=col, cut=cut, ti=tis[0]: (x[:, k, col:col + cut], xB[k][ti]))
                        self.rows_in(src[cut:R, :], R - cut, NCH,
                                     lambda k, col=col, cut=cut, R=R, ti=tis[1]: (x[:, k, col + cut:col + R], xB[k][ti]))
                    col += R
                if G.has_s:
                    ti = xbufs(G.pw, 128)
                    assert len(ti) == 1
                    self.rows_in(a["x_sample"][:, :], 128, NCH, lambda k, ti=ti[0]: (x[:, k, G.pw:G.pw + 128], xB[k][ti]))

                for l in range(DEPTH):
                    if stage >= 4 + 2 * l + 4 * G.idx:
                        self.mixer(G, l, x, xB, h, hB)
                    if stage >= 5 + 2 * l + 4 * G.idx:
                        self.ffn(G, l, x, xB, h, hB)

                with ExitStack() as fes:
                    f = [self.sb("ft", [128, 512], F32, fes) for _ in range(2)]
                    fB = [Buf(), Buf()]
                    sq = [self.sb("sq", [128, 512], BF16, fes) for _ in range(3)]
                    sqB = [Buf() for _ in range(3)]
                    for t in (G.tiles if stage >= 3 else []):
                        n = t.n
                        bS, bSB = self.bank()
                        for c in range(NCH):
                            s_, s_B = sq[c % 3], sqB[c % 3]
                            self.emit(self.ACT, lambda: nc.scalar.activation(out=s_[:, 0:n], in_=x[:, c, t.lo:t.lo + n], func=AF.Square),
                                      reads=[xB[c][t.idx]], writes=[s_B])
                            self.pe_group([dict(out=bS[:, 0:n], lhsT=self.onesD[:, :], rhs=s_[:, 0:n], start=(c == 0), stop=(c == NCH - 1))],
                                          reads=[s_B, self.constB], writes=[bSB])
                        self.emit(self.ACT, lambda: nc.scalar.activation(out=f[0][:, 0:n], in_=bS[:, 0:n], func=AF.Sqrt, bias=self.epsc[:, 0:1]),
                                  reads=[bSB, self.constB], writes=[fB[0]])
                        self.emit(self.DVE, lambda: nc.vector.reciprocal(out=f[1][:, 0:n], in_=f[0][:, 0:n]), reads=[fB[0]], writes=[fB[1]])
                        for c in range(NCH):
                            self.emit(self.DVE, lambda: nc.vector.scalar_tensor_tensor(
                                out=x[:, c, t.lo:t.lo + n], in0=x[:, c, t.lo:t.lo + n], scalar=self.vecs[:, c, V_FG:V_FG + 1],
                                in1=f[1][:, 0:n], op0=ALU.mult, op1=ALU.mult), reads=[xB[c][t.idx], fB[1], self.vecB], writes=[xB[c][t.idx]])
                    col = 0
                    while col < G.pw:
                        pos = G.p0 + col
                        if pos < NMETA:
                            col += NMETA - pos
                            continue
                        R = min(128, G.pw - col)
                        tis = xbufs(col, R)
                        if len(tis) > 1:
                            R = G.tiles[tis[1]].lo - col
                            tis = tis[:1]
                        r0 = pos - NMETA
                        self.rows_out(lambda k, col=col, R=R, ti=tis[0]: (x[:, k, col:col + R], xB[k][ti]), R, NCH,
                                      [(0, R, (lambda c0, ncc, r0=r0, R=R: a["y_prompt"][r0:r0 + R, c0 * 128:(c0 + ncc) * 128]))])
                        col += R
                    if G.has_s:
                        ti = xbufs(G.pw, 128)[0]
                        self.rows_out(lambda k, ti=ti: (x[:, k, G.pw:G.pw + 128], xB[k][ti]), 128, NCH,
                                      [(0, 128, (lambda c0, ncc: a["y_sample"][:, c0 * 128:(c0 + ncc) * 128]))])
                    self.barrier()
        assert stage < 99 or self.w_taken == len(self.plan), (self.w_taken, len(self.plan))
        self.barrier()
        self.es.close()
        return nc


_CACHE = {}


def kernel(x_prompt, x_sample, state_conv, state_pool, state_ffn, meta_tokens,
           norm1_g, w_in, conv_dw, conv_b, ln_g, ln_b, w_conv_out, w_pool, pool_scale,
           w_out, norm2_g, w_up, ffn_dw, w_down, final_g):
    n = 8
    f = lambda v: np.ascontiguousarray(np.asarray(v, dtype=np.float32))
    nc = K().build()
    shared = {
        "meta_tokens": f(meta_tokens), "norm1_g": f(norm1_g), "conv_b": f(conv_b), "ln_g": f(ln_g), "ln_b": f(ln_b),
        "pool_scale": f(pool_scale), "norm2_g": f(norm2_g), "final_g": f(final_g).reshape(1, D),
        "w_in": f(w_in), "conv_dw": f(conv_dw), "w_conv_out": f(w_conv_out), "w_pool": f(w_pool), "w_out": f(w_out),
        "w_up": f(w_up), "ffn_dw": f(ffn_dw), "w_down": f(w_down),
    }
    xp, xs, sc, sp, sf = f(x_prompt), f(x_sample), f(state_conv), f(state_pool), f(state_ffn)
    in_maps = []
    for b in range(n):
        m = dict(shared)
        m["x_prompt"] = xp[b]
        m["x_sample"] = np.ascontiguousarray(xs[NSEQ * b:NSEQ * (b + 1)].reshape(NSEQ * DSEQ, D))
        m["state_conv"] = np.ascontiguousarray(sc[:, NSEQ * b:NSEQ * (b + 1)])
        m["state_pool"] = np.ascontiguousarray(sp[:, NSEQ * b:NSEQ * (b + 1)])
        m["state_ffn"] = np.ascontiguousarray(sf[:, NSEQ * b:NSEQ * (b + 1)])
        in_maps.append(m)
    res = run_bass_kernel_spmd(nc, in_maps, core_ids=list(range(n)))
    r = res.results
    y_prompt = np.stack([r[b]["y_prompt"] for b in range(n)], axis=0)
    y_sample = np.concatenate([r[b]["y_sample"].reshape(NSEQ, DSEQ, D) for b in range(n)], axis=0)
    ncp = np.stack([r[b]["ncp"] for b in range(n)], axis=1)
    npp = np.stack([r[b]["npp"] for b in range(n)], axis=1)
    nfp = np.stack([r[b]["nfp"] for b in range(n)], axis=1)
    ncs = np.concatenate([r[b]["ncs"] for b in range(n)], axis=1)
    nps = np.concatenate([r[b]["nps"] for b in range(n)], axis=1)
    nfs = np.concatenate([r[b]["nfs"] for b in range(n)], axis=1)
    return (y_prompt, y_sample, ncp, npp, nfp, ncs, nps, nfs)
```
